# Optimizing a Trainium2 kernel written in Bass

```python
import math
import jax, jax.numpy as jnp
from jax import lax
import numpy as np

D_MODEL = 1024
BATCH = 2
SEQ = 8192
DEPTH = 1
DEC_BATCH = 128
DEC_SEQ = 4
PAST_LEN = 16384
PAGE_SIZE = 128

N_HEADS = 8
N_KV_HEADS = 2
HEAD_DIM = 64
GQA_GROUP = N_HEADS // N_KV_HEADS
ATTN_WIDTH = N_HEADS * HEAD_DIM
KV_WIDTH = N_KV_HEADS * HEAD_DIM
WINDOW = 128
CHUNK = 128
N_GATE_GROUPS = 4
GMLP_WIDTH = D_MODEL - ATTN_WIDTH
GROUP_CH = GMLP_WIDTH // N_GATE_GROUPS
PROJ_WIDTH = ATTN_WIDTH + 2 * KV_WIDTH + 2 * GMLP_WIDTH
D_FF = 2816
RMS_EPS = 1e-6
FFN_RESIDUAL = 0.5

kernel_name = "hymba_gmlp_swa_sink_macaron_step"


def _rmsnorm(x, g):
    xf = x.astype(jnp.float32)
    y = xf * lax.rsqrt(jnp.mean(xf * xf, axis=-1, keepdims=True) + RMS_EPS)
    return (y * g.astype(jnp.float32)).astype(x.dtype)


def _ffn_half(x, norm, wg, wu, wd):
    h = _rmsnorm(x, norm)
    return x + FFN_RESIDUAL * ((jax.nn.silu(h @ wg) * (h @ wu)) @ wd)


def _alibi_slopes():
    s = 2.0 ** (-8.0 * np.arange(1, N_HEADS + 1, dtype=np.float32) / N_HEADS)
    return jnp.asarray(s.astype(np.float32)).reshape(N_KV_HEADS, GQA_GROUP)


def _project(h, w_in):
    p = h @ w_in
    c1 = ATTN_WIDTH
    c2 = c1 + KV_WIDTH
    c3 = c2 + KV_WIDTH
    c4 = c3 + GMLP_WIDTH
    q, k, v, u, g = jnp.split(p, [c1, c2, c3, c4], axis=-1)
    lead = h.shape[:-1]
    q = q.reshape(*lead, N_KV_HEADS, GQA_GROUP, HEAD_DIM)
    k = k.reshape(*lead, N_KV_HEADS, HEAD_DIM)
    v = v.reshape(*lead, N_KV_HEADS, HEAD_DIM)
    u = jax.nn.gelu(u).reshape(*lead, N_GATE_GROUPS, GROUP_CH)
    g = jax.nn.gelu(g).reshape(*lead, N_GATE_GROUPS, GROUP_CH)
    return q, k, v, u, g


def _sink_attention(q, k, v, dist, valid, sinks):
    slopes = _alibi_slopes()
    s = jnp.einsum('...qhgd,...khd->...hgqk', q, k).astype(jnp.float32) / math.sqrt(HEAD_DIM)
    s = s - slopes[:, :, None, None] * dist.astype(jnp.float32)
    s = jnp.where(valid, s, -jnp.inf)
    sink = jnp.broadcast_to(sinks.astype(jnp.float32).reshape(N_KV_HEADS, GQA_GROUP, 1, 1),
                            s.shape[:-1] + (1,))
    p = jax.nn.softmax(jnp.concatenate([s, sink], axis=-1), axis=-1)[..., :-1]
    o = jnp.einsum('...hgqk,...khd->...qhgd', p.astype(v.dtype), v)
    return o.reshape(*o.shape[:-3], ATTN_WIDTH)


def _spatial_gate(u, g, v_norm, w_s, b_s, n):
    g = _rmsnorm(g, v_norm)
    w = jnp.tril(w_s[:, :n, :n])
    mixed = jnp.einsum('gij,...jgc->...igc', w, g) + b_s[:, :n].T[..., None]
    out = u * mixed
    return out.reshape(*u.shape[:-2], GMLP_WIDTH), g


def _merge(attn, gate, attn_out_norm, gmlp_out_norm, w_out):
    return jnp.concatenate([_rmsnorm(attn, attn_out_norm), _rmsnorm(gate, gmlp_out_norm)], axis=-1) @ w_out


def _mixer_prompt(h, w_in, sinks, v_norm, w_s, b_s, attn_out_norm, gmlp_out_norm, w_out):
    B, T, _ = h.shape
    nb = T // WINDOW
    q, k, v, u, g = _project(h, w_in)
    qb = q.reshape(B, nb, WINDOW, N_KV_HEADS, GQA_GROUP, HEAD_DIM)
    kb = k.reshape(B, nb, WINDOW, N_KV_HEADS, HEAD_DIM)
    vb = v.reshape(B, nb, WINDOW, N_KV_HEADS, HEAD_DIM)
    pad = ((0, 0), (1, 0), (0, 0), (0, 0), (0, 0))
    k_band = jnp.concatenate([jnp.pad(kb, pad)[:, :-1], kb], axis=2)
    v_band = jnp.concatenate([jnp.pad(vb, pad)[:, :-1], vb], axis=2)
    i = jnp.arange(WINDOW)[:, None]
    j = jnp.arange(2 * WINDOW)[None, :]
    dist = WINDOW + i - j
    valid = (dist >= 0) & (dist < WINDOW)
    has_prev = (jnp.arange(nb)[:, None, None] > 0) | (j >= WINDOW)[None]
    valid_full = (valid[None] & has_prev)[:, None, None]
    attn = _sink_attention(qb, k_band, v_band, dist, valid_full, sinks).reshape(B, T, ATTN_WIDTH)
    uc = u.reshape(B, T // CHUNK, CHUNK, N_GATE_GROUPS, GROUP_CH)
    gc = g.reshape(B, T // CHUNK, CHUNK, N_GATE_GROUPS, GROUP_CH)
    gate, _ = _spatial_gate(uc, gc, v_norm, w_s, b_s, CHUNK)
    gate = gate.reshape(B, T, GMLP_WIDTH)
    out = _merge(attn, gate, attn_out_norm, gmlp_out_norm, w_out)
    return out, k[:, T - WINDOW:], v[:, T - WINDOW:]


def _mixer_sample(h, cache_k, cache_v, w_in, sinks, v_norm, w_s, b_s, attn_out_norm, gmlp_out_norm, w_out):
    n = h.shape[1]
    L = cache_k.shape[1]
    q, k, v, u, g = _project(h, w_in)
    k_all = jnp.concatenate([cache_k.astype(k.dtype), k], axis=1)
    v_all = jnp.concatenate([cache_v.astype(v.dtype), v], axis=1)
    i = jnp.arange(n)[:, None]
    j = jnp.arange(L + n)[None, :]
    dist = L + i - j
    valid = (dist >= 0) & (dist < WINDOW)
    attn = _sink_attention(q, k_all, v_all, dist, valid, sinks)
    gate, g_n = _spatial_gate(u, g, v_norm, w_s, b_s, n)
    out = _merge(attn, gate, attn_out_norm, gmlp_out_norm, w_out)
    return out, k_all[:, -L:], v_all[:, -L:], g_n.reshape(h.shape[0], n, GMLP_WIDTH)


def setup_inputs(seed: int = 0) -> dict:
    key = jax.random.key(seed)
    ks = jax.random.split(key, 24)
    cache_rows = min(WINDOW, PAST_LEN)

    def nrm(k, shape, scale):
        return jax.random.normal(k, shape, jnp.float32) * scale

    def gain(k, shape):
        return 1.0 + 0.02 * jax.random.normal(k, shape, jnp.float32)

    return {
        "x_prompt": nrm(ks[0], (BATCH, SEQ, D_MODEL), 1.0),
        "x_sample": nrm(ks[1], (DEC_BATCH, DEC_SEQ, D_MODEL), 1.0),
        "cache_k": nrm(ks[2], (DEPTH, DEC_BATCH, cache_rows, N_KV_HEADS, HEAD_DIM), 1.0),
        "cache_v": nrm(ks[3], (DEPTH, DEC_BATCH, cache_rows, N_KV_HEADS, HEAD_DIM), 1.0),
        "ffn1_norm": gain(ks[4], (DEPTH, D_MODEL)),
        "ffn1_w_gate": nrm(ks[5], (DEPTH, D_MODEL, D_FF), D_MODEL ** -0.5),
        "ffn1_w_up": nrm(ks[6], (DEPTH, D_MODEL, D_FF), D_MODEL ** -0.5),
        "ffn1_w_down": nrm(ks[7], (DEPTH, D_FF, D_MODEL), D_FF ** -0.5),
        "mix_norm": gain(ks[8], (DEPTH, D_MODEL)),
        "w_in": nrm(ks[9], (DEPTH, D_MODEL, PROJ_WIDTH), D_MODEL ** -0.5),
        "attn_sinks": nrm(ks[10], (DEPTH, N_HEADS), 1.0),
        "gmlp_v_norm": gain(ks[11], (DEPTH, N_GATE_GROUPS, GROUP_CH)),
        "gmlp_w_spatial": nrm(ks[12], (DEPTH, N_GATE_GROUPS, CHUNK, CHUNK), 0.5 * CHUNK ** -0.5),
        "gmlp_b_spatial": gain(ks[13], (DEPTH, N_GATE_GROUPS, CHUNK)),
        "attn_out_norm": gain(ks[14], (DEPTH, ATTN_WIDTH)),
        "gmlp_out_norm": gain(ks[15], (DEPTH, GMLP_WIDTH)),
        "w_out": nrm(ks[16], (DEPTH, D_MODEL, D_MODEL), D_MODEL ** -0.5),
        "ffn2_norm": gain(ks[17], (DEPTH, D_MODEL)),
        "ffn2_w_gate": nrm(ks[18], (DEPTH, D_MODEL, D_FF), D_MODEL ** -0.5),
        "ffn2_w_up": nrm(ks[19], (DEPTH, D_MODEL, D_FF), D_MODEL ** -0.5),
        "ffn2_w_down": nrm(ks[20], (DEPTH, D_FF, D_MODEL), D_FF ** -0.5),
        "final_norm": gain(ks[21], (D_MODEL,)),
    }


def reference(x_prompt, x_sample, cache_k, cache_v, ffn1_norm, ffn1_w_gate, ffn1_w_up, ffn1_w_down,
              mix_norm, w_in, attn_sinks, gmlp_v_norm, gmlp_w_spatial, gmlp_b_spatial,
              attn_out_norm, gmlp_out_norm, w_out, ffn2_norm, ffn2_w_gate, ffn2_w_up, ffn2_w_down,
              final_norm):
    xp, xs = x_prompt, x_sample
    pk, pv, sk, sv, sc = [], [], [], [], []
    for l in range(DEPTH):
        xp = _ffn_half(xp, ffn1_norm[l], ffn1_w_gate[l], ffn1_w_up[l], ffn1_w_down[l])
        xs = _ffn_half(xs, ffn1_norm[l], ffn1_w_gate[l], ffn1_w_up[l], ffn1_w_down[l])
        mp, kp, vp = _mixer_prompt(_rmsnorm(xp, mix_norm[l]), w_in[l], attn_sinks[l], gmlp_v_norm[l],
                                   gmlp_w_spatial[l], gmlp_b_spatial[l], attn_out_norm[l],
                                   gmlp_out_norm[l], w_out[l])
        ms, ks_, vs_, cs = _mixer_sample(_rmsnorm(xs, mix_norm[l]), cache_k[l], cache_v[l], w_in[l],
                                         attn_sinks[l], gmlp_v_norm[l], gmlp_w_spatial[l],
                                         gmlp_b_spatial[l], attn_out_norm[l], gmlp_out_norm[l], w_out[l])
        xp = xp + mp
        xs = xs + ms
        xp = _ffn_half(xp, ffn2_norm[l], ffn2_w_gate[l], ffn2_w_up[l], ffn2_w_down[l])
        xs = _ffn_half(xs, ffn2_norm[l], ffn2_w_gate[l], ffn2_w_up[l], ffn2_w_down[l])
        pk.append(kp); pv.append(vp); sk.append(ks_); sv.append(vs_); sc.append(cs)
    y_prompt = _rmsnorm(xp, final_norm)
    y_sample = _rmsnorm(xs, final_norm)
    prompt_k = jnp.stack(pk, axis=0)
    prompt_v = jnp.stack(pv, axis=0)
    sample_k = jnp.stack(sk, axis=0)
    sample_v = jnp.stack(sv, axis=0)
    sample_chunk_v = jnp.stack(sc, axis=0)
    return (y_prompt, y_sample, prompt_k, prompt_v, sample_k, sample_v, sample_chunk_v)
```

```python
import os
import numpy as np
import concourse.bass as bass
import concourse.mybir as mybir
from concourse.bass_utils import run_bass_kernel_spmd
from contextlib import ExitStack

F32 = mybir.dt.float32
BF16 = mybir.dt.bfloat16
AF = mybir.ActivationFunctionType
ALU = mybir.AluOpType

D = 1024
DFF = 2816
NJ = 22
NSLOT = 14
PARTS = [(0, 6), (6, 11), (11, 17), (17, 22)]
EPS = 1e-6
NCORES = 8
PIECES_PER_FFN = 66
GELU_C = 0.7978845608028654


class Res:
    __slots__ = ("name", "w", "r", "rd")

    def __init__(self, name=""):
        self.name = name
        self.w = None
        self.r = {}
        self.rd = []


class Op:
    __slots__ = ("eng", "fn", "deps", "sem", "signal", "val")

    def __init__(self, eng, fn, deps, sem=None):
        self.eng = eng
        self.fn = fn
        self.deps = deps
        self.sem = sem
        self.signal = sem is not None
        self.val = None


class Prog:
    ENGS = ("pe", "act", "dve", "pool", "sp")

    def __init__(self):
        self.lists = {e: [] for e in self.ENGS}

    def op(self, eng, fn, reads=(), writes=(), dma=None):
        deps = []
        for r in reads:
            if r.w is not None:
                deps.append(r.w)
        for w in writes:
            if w.w is not None:
                deps.append(w.w)
            deps.extend(w.r.values())
            deps.extend(w.rd)
        o = Op(eng, fn, deps, sem=dma)
        for r in reads:
            if dma is None:
                r.r[eng] = o
            else:
                r.rd.append(o)
        for w in writes:
            w.w = o
            w.r = {}
            w.rd = []
        for d in deps:
            if d.sem is None and not (d.eng == "pe" and eng == "pe" and dma is None):
                d.signal = True
        self.lists[eng].append(o)
        return o

    def emit(self, nc, stack):
        eng_sem = {e: stack.enter_context(nc.semaphore("s_" + e)) for e in self.ENGS}
        dsem = {}
        dcount = {}
        for e in self.ENGS:
            cnt = 0
            for o in self.lists[e]:
                if o.sem is None:
                    if o.signal:
                        cnt += 1
                        o.val = cnt
                else:
                    if o.sem not in dsem:
                        dsem[o.sem] = stack.enter_context(nc.semaphore("d_" + o.sem))
                        dcount[o.sem] = 0
                    dcount[o.sem] += 16
                    o.val = dcount[o.sem]
        block = stack.enter_context(nc.Block())
        lists = self.lists

        def run(e, eng):
            waited = {}
            for o in lists[e]:
                need = {}
                for d in o.deps:
                    if d.sem is None:
                        if d.eng == "pe" and e == "pe" and o.sem is None:
                            continue
                        key = ("e", d.eng)
                        s = eng_sem[d.eng]
                    else:
                        key = ("d", d.sem)
                        s = dsem[d.sem]
                    v = d.val
                    if waited.get(key, 0) >= v:
                        continue
                    if key not in need or need[key][1] < v:
                        need[key] = (s, v)
                for key, (s, v) in need.items():
                    eng.wait_ge(s, v)
                    waited[key] = v
                ins = o.fn(eng)
                if o.signal:
                    if o.sem is None:
                        ins.then_inc(eng_sem[e], 1)
                    else:
                        ins.then_inc(dsem[o.sem], 16)
            if e == "sp":
                for name, s in dsem.items():
                    eng.wait_ge(s, dcount[name])

        @block.tensor
        def _(eng):
            run("pe", eng)

        @block.scalar
        def _(eng):
            run("act", eng)

        @block.vector
        def _(eng):
            run("dve", eng)

        @block.gpsimd
        def _(eng):
            run("pool", eng)

        @block.sync
        def _(eng):
            run("sp", eng)


def MM(out, lhsT, rhs, start=True, stop=True):
    return lambda e: e.matmul(out, lhsT=lhsT, rhs=rhs, start=start, stop=stop)


def TR(out, in_, ident):
    return lambda e: e.transpose(out, in_, ident)


def ACT(out, in_, func, **kw):
    return lambda e: e.activation(out=out, in_=in_, func=func, **kw)


def TT(out, in0, in1, op):
    return lambda e: e.tensor_tensor(out=out, in0=in0, in1=in1, op=op)


def STT(out, in0, scalar, in1, op0, op1):
    return lambda e: e.scalar_tensor_tensor(out=out, in0=in0, scalar=scalar, in1=in1, op0=op0, op1=op1)


def TS(out, in0, s1, s2, op0, op1):
    return lambda e: e.tensor_scalar(out=out, in0=in0, scalar1=s1, scalar2=s2, op0=op0, op1=op1)


def CP(out, in_):
    return lambda e: e.tensor_copy(out=out, in_=in_)


def RCP(out, in_):
    return lambda e: e.reciprocal(out=out, in_=in_)


def MS(ap, v):
    return lambda e: e.memset(ap, v)


def DMA(out, in_):
    return lambda e: e.dma_start(out=out, in_=in_)

def build_program():
    nc = bass.Bass("TRN2", target_bir_lowering=False)

    def din(name, shape):
        return nc.dram_tensor(name, list(shape), F32, kind="ExternalInput").ap()

    def dout(name, shape):
        return nc.dram_tensor(name, list(shape), F32, kind="ExternalOutput").ap()

    xin_d = din("xin", [2240, D])
    wffn_d = din("wffn", [2 * PIECES_PER_FFN, 128, 1024])
    wqk_d = din("wqk", [128, 8 * 640])
    wtok_d = din("wtok", [128, 8 * 1280])
    wout_d = din("wout", [128, 8 * 1024])
    gains_d = din("gains", [5, 128, 1024])
    vnb_d = din("vnb", [128, 512])
    wsp_d = din("wsp", [128, 512])
    tril_d = din("tril", [128, 128])
    wsps_d = din("wsps", [64, 256])
    masks_d = din("masks", [64, 64])
    bsp_d = din("bsp", [128, 4])
    bsps_d = din("bsps", [64, 4])
    sinks_d = din("sinks", [128, 8])
    flag_d = din("flag", [128, 1])
    bm_d = din("bm", [128, 2 * 8 * 128])
    bmf_d = din("bmf", [128, 8 * 128])
    ebc_d = din("ebc", [128, 512])
    ebn_d = din("ebn", [64, 512])
    ident_d = din("ident", [128, 128])
    ck_d = din("ck", [16, 128, 128])
    cv_d = din("cv", [16, 128, 128])

    yout_d = dout("yout", [2112, D])
    pk_d = dout("pk", [128, 128])
    pv_d = dout("pv", [128, 128])
    sk_d = dout("sk", [16, 128, 128])
    sv_d = dout("sv", [16, 128, 128])
    scv_d = dout("scv", [64, 512])

    P = Prog()
    with ExitStack() as st:
        def sb(name, shape, dt):
            return st.enter_context(nc.sbuf_tensor("sb_" + name, list(shape), dt))

        xt = sb("xt", [128, 6, D], F32)
        hT = sb("hT", [128, 8, 768], BF16)
        aT = sb("aT", [128, 6, 768], BF16)
        ring = [sb(f"ring{i}", [128, 1024], BF16) for i in range(NSLOT)]
        wqk = sb("wqk", [128, 8, 640], BF16)
        wtok = sb("wtok", [128, 8, 1280], BF16)
        wout = sb("wout", [128, 8, 1024], BF16)
        bm = sb("bm", [128, 2, 8, 128], BF16)
        bmf = sb("bmf", [128, 8, 128], BF16)
        ebc = sb("ebc", [128, 512], F32)
        ebn = sb("ebn", [64, 512], F32)
        gains = sb("gains", [128, 5, D], F32)
        vnb = sb("vnb", [128, 4, 128], F32)
        hb = [sb(f"hb{i}", [128, D], BF16) for i in range(2)]
        ftmp = sb("ftmp", [128, 4, 512], F32)
        gnb = sb("gnb", [128, 4, 128], BF16)
        ebuf = [sb(f"ebuf{i}", [128, 512], F32) for i in range(2)]
        PT = [sb(f"PT{i}", [128, 512], BF16) for i in range(4)]
        class BufSet:
            pass

        sets = []
        for pz in range(2):
            bs = BufSet()
            bs.u2 = sb(f"u2_{pz}", [128, 4, 128], F32)
            bs.g2 = sb(f"g2_{pz}", [128, 4, 128], F32)
            bs.u2f = bs.u2[:, :, :].rearrange("p a b -> p (a b)")
            bs.g2f = bs.g2[:, :, :].rearrange("p a b -> p (a b)")
            bs.mraw = sb(f"mraw_{pz}", [128, D], F32)
            bs.m = sb(f"m_{pz}", [128, D], BF16)
            bs.mT = sb(f"mT_{pz}", [128, 8, 128], BF16)
            bs.qTz = sb(f"qTz_{pz}", [128, 2, 512], BF16)
            bs.qT = bs.qTz[:, 0, :]
            for k in ["u2", "g2", "attn", "gate2", "m_a", "m_g", "mT", "qT"]:
                setattr(bs, "R_" + k, Res(f"{k}_{pz}"))
            sets.append(bs)

        class _CB:
            def set(self, pz):
                self.__dict__.update(sets[pz].__dict__)

        CB = _CB()
        CB.set(0)
        kT = [sb(f"kT{i}", [128, 128], BF16) for i in range(3)]
        Va = [sb(f"Va{i}", [128, 2, 65], BF16) for i in range(3)]
        kvout = sb("kvout", [128, 256], F32)
        otsb = sb("otsb", [128, 512], F32)
        gn32 = otsb[:, :].rearrange("p (a b) -> p a b", a=4)
        WtT = sb("WtT", [128, 4, 128], BF16)
        BD = sb("BD", [64, 4, 64], BF16)
        ident_f = sb("ident_f", [128, 128], F32)
        ident_b = sb("ident_b", [128, 128], BF16)
        stat = sb("stat", [128, 80], F32)
        nst = sb("nst", [128, 4, 16], F32)
        mh = sb("mh", [128, 16], F32)
        bsp = sb("bsp", [128, 4], F32)
        bsps = sb("bsps", [64, 4], F32)
        esink = sb("esink", [128, 8], F32)
        flag = sb("flag", [128, 1], F32)

        aT_flat = aT[:, :, :].rearrange("p a b -> p (a b)")
        kTc = aT_flat[:, 0:2048].rearrange("p (b r) -> p b r", b=16)
        cvb_flat = aT_flat[:, 2048:2048 + 2080]
        cvb = cvb_flat.rearrange("p (b k d) -> p b k d", b=16, k=2)
        ftmp_b = ftmp.bitcast(BF16)
        ckb = ftmp_b[:, 0:2, :].rearrange("p a b -> p (a b)").rearrange("p (b c) -> p b c", b=16)
        cvraw = ftmp_b[:, 2:4, :].rearrange("p a b -> p (a b)").rearrange("p (b c) -> p b c", b=16)

        banks = [st.enter_context(nc.psum_tensor(f"ps{i}", [128, 512], F32)) for i in range(8)]
        banks_b = [b.bitcast(BF16) for b in banks]
        RB = [Res(f"bank{i}") for i in range(8)]
        bank_ctr = [0]

        def psum(i=None):
            if i is None:
                i = bank_ctr[0] % 8
                bank_ctr[0] += 1
            return banks[i], banks_b[i], RB[i]

        RX = [Res(f"x{i}") for i in range(6)]
        RH = [Res(f"h{i}") for i in range(6)]
        RA = [[Res(f"a{j}_{s}") for s in range(2)] for j in range(6)]
        RRING = [Res(f"ring{i}") for i in range(NSLOT)]
        RHB = [Res("hb0"), Res("hb1")]
        RF = [Res(f"ftmp{i}") for i in range(4)]
        RC = Res("consts")
        R = {k: Res(k) for k in ["u2", "g2", "gnb", "gn32", "attn", "gate2", "m_a", "m_g", "mT", "ebuf0", "ebuf1",
                                 "PT0", "PT1", "PT2", "PT3", "qT", "kT0", "kT1", "kT2", "Va0", "Va1", "Va2",
                                 "kvout", "otsb", "WtT", "BD", "ident_b", "esink", "mh", "wqk", "wtok", "wout",
                                 "dn", "bm"]}
        RSTAT = [Res(f"stat{i}") for i in range(16)]
        stat_ctr = [0]
        dn8 = stat[:, 64:72]

        def newstat():
            i = stat_ctr[0] % 16
            stat_ctr[0] += 1
            return stat[:, i * 4:(i + 1) * 4], RSTAT[i]

        P.op("sp", DMA(ident_f[:, :], ident_d), writes=[RC], dma="c_ident")
        P.op("sp", DMA(gains[:, 0, :], gains_d[0]), writes=[RC], dma="c_gains0")

        def late_init():
            P.op("sp", DMA(gains[:, 1:5, :], gains_d[1:5].rearrange("g p d -> p g d")), writes=[RC], dma="c_gains")
            P.op("sp", DMA(vnb[:, :, :], vnb_d.rearrange("p (g c) -> p g c", g=4)), writes=[RC], dma="c_vnb")
            P.op("pool", DMA(bm[:, :, :, :], bm_d.rearrange("p (a h q) -> p a h q", a=2, h=8)), writes=[R["bm"]], dma="c_bm")
            P.op("pool", DMA(bmf[:, :, :], bmf_d.rearrange("p (h q) -> p h q", h=8)), writes=[R["bm"]], dma="c_bmf")
            P.op("sp", DMA(ebc[:, :], ebc_d), writes=[RC], dma="c_ebc")
            P.op("sp", DMA(ebn[:, :], ebn_d), writes=[RC], dma="c_ebn")
            P.op("sp", DMA(bsp[:, :], bsp_d), writes=[RC], dma="c_bsp")
            P.op("sp", DMA(bsps[:, :], bsps_d), writes=[RC], dma="c_bsps")
            P.op("sp", DMA(esink[:, :], sinks_d), writes=[R["esink"]], dma="c_sinks")
            P.op("sp", DMA(flag[:, :], flag_d), writes=[RC], dma="c_flag")
            stg = sets[1].mraw
            rstg = sets[1].R_attn
            P.op("sp", DMA(stg[:, 0:512], wsp_d), writes=[rstg], dma="c_wsp")
            P.op("sp", DMA(stg[:, 512:640], tril_d), writes=[rstg], dma="c_tril")
            P.op("sp", DMA(stg[0:64, 640:896], wsps_d), writes=[rstg], dma="c_wsps")
            P.op("sp", DMA(stg[0:64, 896:960], masks_d), writes=[rstg], dma="c_masks")
            wqk_v = wqk_d.rearrange("p (a b) -> p a b", a=8)
            wtok_v = wtok_d.rearrange("p (a b) -> p a b", a=8)
            wout_v = wout_d.rearrange("p (a b) -> p a b", a=8)
            for dc in range(0, 8, 2):
                bg_dma.append(lambda dc=dc: P.op("pool", DMA(wtok[:, dc:dc + 2, :], wtok_v[:, dc:dc + 2, :]), writes=[R["wtok"]], dma="c_wtok"))
            for dc in range(0, 8, 4):
                bg_dma.append(lambda dc=dc: P.op("pool", DMA(wqk[:, dc:dc + 4, :], wqk_v[:, dc:dc + 4, :]), writes=[R["wqk"]], dma="c_wqk"))
            for dc in range(0, 8, 2):
                bg_dma.append(lambda dc=dc: P.op("pool", DMA(wout[:, dc:dc + 2, :], wout_v[:, dc:dc + 2, :]), writes=[R["wout"]], dma="c_wout"))

        def late_init2():
            pass

        def late_copies():
            P.op("pool", DMA(sk_d[:, 0:124, :], ck_d[:, 4:128, :]), dma="o_skc")
            P.op("pool", DMA(sv_d[:, 0:124, :], cv_d[:, 4:128, :]), dma="o_svc")

        def late_init_compute():
            stg = sets[1].mraw
            rstg = sets[1].R_attn
            P.op("act", ACT(esink[:, :], esink[:, :], AF.Exp), reads=[R["esink"]], writes=[R["esink"]])
            for G in range(4):
                P.op("dve", TT(WtT[:, G, :], stg[:, G * 128:(G + 1) * 128], stg[:, 512:640], ALU.mult),
                     reads=[rstg], writes=[R["WtT"]])
                P.op("dve", TT(BD[:, G, :], stg[0:64, 640 + G * 64:640 + (G + 1) * 64], stg[0:64, 896:960], ALU.mult),
                     reads=[rstg], writes=[R["BD"]])

        for pz in range(2):
            P.op("pool", MS(sets[pz].qTz[:, :, :], 0.0), writes=[sets[pz].R_qT])
        P.op("pool", MS(mh[:, 0:8], -0.5), writes=[R["mh"]])
        P.op("pool", MS(mh[:, 8:9], EPS), writes=[R["mh"]])
        P.op("pool", MS(mh[:, 9:10], EPS), writes=[R["mh"]])
        epsv = mh[:, 8:10]
        for i in range(3):
            P.op("pool", MS(Va[i][:, :, :], 1.0), writes=[R[f"Va{i}"]])
        P.op("dve", CP(ident_b[:, :], ident_f[:, :]), reads=[RC], writes=[R["ident_b"]])
        TOTAL_PIECES = 3 * 2 * PIECES_PER_FFN
        ring_state = {"issued": 0, "next": 0}

        def ring_issue(upto):
            while ring_state["issued"] < min(upto, TOTAL_PIECES):
                k = ring_state["issued"]
                s = k % NSLOT
                extra = list(RX) if (4 <= k < NSLOT) else []
                P.op("pool", DMA(ring[s][:, :], wffn_d[k % (2 * PIECES_PER_FFN)]), reads=extra, writes=[RRING[s]], dma=f"ring{s}")
                ring_state["issued"] += 1

        def ring_get():
            k = ring_state["next"]
            ring_state["next"] += 1
            assert k < ring_state["issued"], "ring underflow"
            s = k % NSLOT
            return ring[s], RRING[s]

        bg_dma = []

        bg_ctr = [0]

        def ring_done():
            ring_issue(ring_state["next"] + NSLOT)
            bg_ctr[0] += 1
            if bg_dma and bg_ctr[0] % 2 == 0:
                bg_dma.pop(0)()

        ring_issue(4)

        def rstd_from_ss(ss_ap, out_ap, res, n, inv_count, eps):
            w = out_ap.shape[1]
            P.op("dve", TS(out_ap, ss_ap, inv_count, eps, ALU.mult, ALU.add), reads=[res], writes=[res])
            P.op("pool", TT(out_ap, out_ap, mh[:n, 0:w], ALU.pow), reads=[res, R["mh"]], writes=[res])

        hb_ctr = [0]
        RNST = [Res(f"nst{i}") for i in range(4)]
        nst_ctr = [0]
        junk_ctr = [0]

        P.op("pool", MS(nst[:, :, :], 1.0), writes=RNST)

        def norm_stats(blocks, ncols):
            i = nst_ctr[0] % 4
            nst_ctr[0] += 1
            t, rt = nst[:, i, :], RNST[i]
            for k, lb in enumerate(blocks):
                n = ncols[lb]
                j = junk_ctr[0] % 2
                junk_ctr[0] += 1
                jk = ftmp_b[:, 2 + j, :]
                P.op("act", ACT(jk[:n, :], xt[:n, lb, :], AF.Square, accum_out=t[:n, k:k + 1]),
                     reads=[RX[lb]], writes=[RF[2 + j], rt])
            nb = len(blocks)
            P.op("dve", TS(t[:, 8:8 + nb], t[:, 0:nb], 1.0 / D, EPS, ALU.mult, ALU.add), reads=[rt], writes=[rt])
            P.op("pool", TT(t[:, 8:8 + nb], t[:, 8:8 + nb], mh[:, 0:nb], ALU.pow),
                 reads=[rt, R["mh"]], writes=[rt])
            return t, rt

        def norm_phase(blocks, ncols, gidx):
            t, rt = norm_stats(blocks, ncols)
            for k, lb in enumerate(blocks):
                n = ncols[lb]
                c0 = lb * 128
                i = hb_ctr[0] % 2
                hb_ctr[0] += 1
                hbt, rhb = hb[i], RHB[i]
                P.op("dve", STT(hbt[:n, :], xt[:n, lb, :], t[:n, 8 + k:9 + k], gains[:n, gidx, :], ALU.mult, ALU.mult),
                     reads=[RX[lb], rt, RC], writes=[rhb])
                _, pb, rb = psum()
                for dc in range(8):
                    P.op("pe", TR(pb[:, dc * n:(dc + 1) * n], hbt[:n, dc * 128:(dc + 1) * 128], ident_b[:n, :n]),
                         reads=[rhb, R["ident_b"]], writes=[rb])
                P.op("act", ACT(hT[:, :, c0:c0 + n], pb[:, 0:8 * n].rearrange("p (a b) -> p a b", a=8), AF.Copy),
                     reads=[rb], writes=[RH[lb]])

        def norm_transpose(lb, n, gidx, c0):
            i = hb_ctr[0] % 2
            hb_ctr[0] += 1
            hbt, rhb = hb[i], RHB[i]
            stt, rs = newstat()
            P.op("act", ACT(hbt[:n, :], xt[:n, lb, :], AF.Square, accum_out=stt[:n, 0:1]),
                 reads=[RX[lb]], writes=[rhb, rs])
            rstd_from_ss(stt[:n, 0:1], stt[:n, 1:2], rs, n, 1.0 / D, EPS)
            P.op("dve", STT(hbt[:n, :], xt[:n, lb, :], stt[:n, 1:2], gains[:n, gidx, :], ALU.mult, ALU.mult),
                 reads=[RX[lb], rs, RC], writes=[rhb])
            _, pb, rb = psum()
            for dc in range(8):
                P.op("pe", TR(pb[:, dc * n:(dc + 1) * n], hbt[:n, dc * 128:(dc + 1) * 128], ident_b[:n, :n]),
                     reads=[rhb, R["ident_b"]], writes=[rb])
            P.op("act", ACT(hT[:, :, c0:c0 + n], pb[:, 0:8 * n].rearrange("p (a b) -> p a b", a=8), AF.Copy),
                 reads=[rb], writes=[RH[lb]])

        def subs_of(blocks, ncols):
            c_start = blocks[0] * 128
            c_end = blocks[-1] * 128 + ncols[blocks[-1]]
            out = []
            c = c_start
            while c < c_end:
                ce = min(c + 512, c_end)
                lbs = [lb for lb in blocks if lb * 128 < ce and lb * 128 + ncols[lb] > c]
                out.append((c, ce, lbs))
                c = ce
            return out

        ft_ctr = [0]

        part_hook = []

        def ffn(blocks, ncols, gidx, on_block_final=None, sub_major_first=False, after_first_norm=None, skip_last_ring_done=False,
                prenormed=()):
            pend_final = []
            for (c0_, c1_, lbs_) in subs_of(blocks, ncols):
                grp = [lb for lb in lbs_ if lb * 128 >= c0_ and lb not in prenormed]
                if grp:
                    norm_phase(grp, ncols, gidx)
                if after_first_norm is not None:
                    after_first_norm()
                    after_first_norm = None
            KFFN = os.environ.get("KFFN", "")
            if KFFN == "n":
                return
            subs = subs_of(blocks, ncols)
            sub_of_lb = {}
            for si, (c0, c1, lbs) in enumerate(subs):
                for lb in lbs:
                    sub_of_lb.setdefault(lb, []).append(si)
            for (ja, jb) in PARTS:
                npart = jb - ja
                def gate_up(jj, si, wg, rwg, wu, rwu):
                    c0, c1, lbs = subs[si]
                    w = c1 - c0
                    gps, _, rg = psum()
                    ups, _, ru = psum()
                    rh = [RH[lb] for lb in lbs]
                    for dc in range(8):
                        P.op("pe", MM(gps[:, 0:w], wg[:, dc * 128:(dc + 1) * 128], hT[:, dc, c0:c1],
                                      start=(dc == 0), stop=(dc == 7)), reads=[rwg] + rh, writes=[rg])
                    for dc in range(8):
                        P.op("pe", MM(ups[:, 0:w], wu[:, dc * 128:(dc + 1) * 128], hT[:, dc, c0:c1],
                                      start=(dc == 0), stop=(dc == 7)), reads=[rwu] + rh, writes=[ru])
                    fi = ft_ctr[0] % 4
                    ft_ctr[0] += 1
                    t = ftmp[:, fi, 0:w]
                    rt = RF[fi]
                    P.op("act", ACT(t, gps[:, 0:w], AF.Tanh, scale=0.5), reads=[rg], writes=[rt])
                    P.op("dve", STT(t, t, 1.0, gps[:, 0:w], ALU.add, ALU.mult), reads=[rt, rg], writes=[rt])
                    P.op("dve", TT(aT[:, jj, c0:c1], t, ups[:, 0:w], ALU.mult), reads=[rt, ru], writes=[RA[jj][si]])

                if sub_major_first and (ja, jb) == PARTS[0] and len(subs) == 2 and 2 * npart <= NSLOT:
                    base = ring_state["next"]
                    pcs = [(ring_get(), ring_get()) for _ in range(npart)]
                    for jj in range(npart):
                        (wg, rwg), (wu, rwu) = pcs[jj]
                        gate_up(jj, 0, wg, rwg, wu, rwu)
                    for jj in range(npart):
                        (wg, rwg), (wu, rwu) = pcs[jj]
                        gate_up(jj, 1, wg, rwg, wu, rwu)
                        ring_issue(base + 2 * (jj + 1) + NSLOT)
                        bg_ctr[0] += 1
                        if bg_dma and bg_ctr[0] % 2 == 0:
                            bg_dma.pop(0)()
                else:
                    for jj in range(npart):
                        wg, rwg = ring_get()
                        wu, rwu = ring_get()
                        for si in range(len(subs)):
                            gate_up(jj, si, wg, rwg, wu, rwu)
                        ring_done()
                        if KFFN == "g1":
                            return
                if KFFN == "g":
                    return
                wds = [ring_get() for _ in range(npart)]
                for lb in blocks:
                    n = ncols[lb]
                    for half in range(2):
                        yps, _, ry = psum()
                        for jj in range(npart):
                            wd, rwd = wds[jj]
                            P.op("pe", MM(yps[:n, :], aT[:, jj, lb * 128:lb * 128 + n], wd[:, half * 512:(half + 1) * 512],
                                          start=(jj == 0), stop=(jj == npart - 1)),
                                 reads=[rwd] + [RA[jj][si] for si in sub_of_lb[lb]], writes=[ry])
                        xs = xt[:n, lb, half * 512:(half + 1) * 512]
                        P.op("dve", STT(xs, yps[:n, :], 0.25, xs, ALU.mult, ALU.add), reads=[ry, RX[lb]], writes=[RX[lb]])
                    if on_block_final is not None and (ja, jb) == PARTS[-1]:
                        pend_final.append(lb)
                        if len(pend_final) > 1:
                            on_block_final(pend_final.pop(0))
                if on_block_final is not None and (ja, jb) == PARTS[-1]:
                    while pend_final:
                        on_block_final(pend_final.pop(0))
                if not (skip_last_ring_done and (ja, jb) == PARTS[-1]):
                    ring_done()
                if part_hook:
                    part_hook.pop(0)()
                if KFFN == "d":
                    return

        def gelu2(ps, rps, dst, rdst):
            P.op("act", ACT(dst, ps, AF.Gelu_apprx_tanh), reads=[rps], writes=[rdst])

        def mixer_tail(lb, n, oA, rA, oB, rB, bpb=None, bops=((None, None), (None, None))):
            rd8 = R["dn"]
            for kvh, (o, ro) in enumerate(((oA, rA), (oB, rB))):
                ov = o[:n, 0:260].rearrange("p (g c) -> p g c", g=4)
                P.op("dve", TT(dn8[:n, kvh * 4:(kvh + 1) * 4], ov[:, :, 64], esink[:n, kvh * 4:(kvh + 1) * 4], ALU.add),
                     reads=[ro, R["esink"]], writes=[rd8])
            P.op("dve", RCP(dn8[:n, :], dn8[:n, :]), reads=[rd8], writes=[rd8])
            for kvh, (o, ro) in enumerate(((oA, rA), (oB, rB))):
                ov = o[:n, 0:260].rearrange("p (g c) -> p g c", g=4)[:, :, 0:64]
                dst = CB.mraw[:n, kvh * 256:(kvh + 1) * 256].rearrange("p (g c) -> p g c", g=4)
                bc = dn8[:n, kvh * 4:(kvh + 1) * 4].unsqueeze(2).to_broadcast([n, 4, 64])
                P.op("dve", TT(dst, ov, bc, ALU.mult), reads=[ro, rd8], writes=[CB.R_attn])
            yield
            P.op("dve", TT(CB.m[:n, 0:512], CB.mraw[:n, 0:512], gains[:n, 4, 0:512], ALU.mult),
                 reads=[CB.R_attn, RC], writes=[CB.R_m_a])
            P.op("dve", TT(CB.m[:n, 512:1024], CB.mraw[:n, 512:1024], gains[:n, 4, 512:1024], ALU.mult),
                 reads=[CB.R_gate2, RC], writes=[CB.R_m_g])
            yield
            _, pb, rb = psum(bpb)
            for cc in range(8):
                P.op("pe", TR(pb[:, cc * n:(cc + 1) * n], CB.m[:n, cc * 128:(cc + 1) * 128], ident_b[:n, :n]),
                     reads=[CB.R_m_a, CB.R_m_g, R["ident_b"]], writes=[rb])
            P.op("act", ACT(CB.mT[:, :, 0:n], pb[:, 0:8 * n].rearrange("p (a b) -> p a b", a=8), AF.Copy),
                 reads=[rb], writes=[CB.R_mT])
            yield
            sa, rsa = newstat()
            P.op("act", ACT(CB.mraw[:n, 0:512], CB.mraw[:n, 0:512], AF.Square, accum_out=sa[:n, 0:1]),
                 reads=[CB.R_attn], writes=[CB.R_attn, rsa])
            P.op("act", ACT(CB.mraw[:n, 512:1024], CB.mraw[:n, 512:1024], AF.Square, accum_out=sa[:n, 1:2]),
                 reads=[CB.R_gate2], writes=[CB.R_gate2, rsa])
            P.op("dve", STT(sa[:n, 2:4], sa[:n, 0:2], 1.0 / 512, epsv[:n, 0:2], ALU.mult, ALU.add), reads=[rsa, R["mh"]], writes=[rsa])
            P.op("pool", TT(sa[:n, 2:4], sa[:n, 2:4], mh[:n, 0:2], ALU.pow), reads=[rsa, R["mh"]], writes=[rsa])
            for half in range(2):
                yield
                opa, _, ropa = psum(bops[half][0])
                opg, _, ropg = psum(bops[half][1])
                for cc in range(4):
                    P.op("pe", MM(opa[:n, :], CB.mT[:, cc, 0:n], wout[:, cc, half * 512:(half + 1) * 512],
                                  start=(cc == 0), stop=(cc == 3)), reads=[CB.R_mT, R["wout"]], writes=[ropa])
                for cc in range(4, 8):
                    P.op("pe", MM(opg[:n, :], CB.mT[:, cc, 0:n], wout[:, cc, half * 512:(half + 1) * 512],
                                  start=(cc == 4), stop=(cc == 7)), reads=[CB.R_mT, R["wout"]], writes=[ropg])
                xs = xt[:n, lb, half * 512:(half + 1) * 512]
                P.op("dve", STT(xs, opa[:n, :], sa[:n, 2:3], xs, ALU.mult, ALU.add), reads=[ropa, rsa, RX[lb]], writes=[RX[lb]])
                P.op("dve", STT(xs, opg[:n, :], sa[:n, 3:4], xs, ALU.mult, ALU.add), reads=[ropg, rsa, RX[lb]], writes=[RX[lb]])

        def proj_qk(lb, n, c0, kbuf, rk, want_q=True, perm_q=False, bq=None, bk=None):
            rh = [RH[lb], R["wqk"]]
            if want_q:
                qps, _, rq = psum(bq)
                for cb in range(4):
                    for dc in range(8):
                        P.op("pe", MM(qps[:, cb * n:(cb + 1) * n], wqk[:, dc, cb * 128:(cb + 1) * 128],
                                      hT[:, dc, c0:c0 + n], start=(dc == 0), stop=(dc == 7)), reads=rh, writes=[rq])
                if perm_q:
                    P.op("act", ACT(CB.qT[:, 0:256].rearrange("p (b g i) -> p g b i", b=16, g=4),
                                    qps[:, 0:256].rearrange("p (g b i) -> p g b i", g=4, b=16), AF.Copy),
                         reads=[rq], writes=[CB.R_qT])
                else:
                    P.op("act", ACT(CB.qTz[0:64, 0, :], qps[0:64, 0:512], AF.Copy), reads=[rq], writes=[CB.R_qT])
                    P.op("act", ACT(CB.qTz[64:128, 1, :], qps[64:128, 0:512], AF.Copy), reads=[rq], writes=[CB.R_qT])
            kps, _, rkp = psum(bk)
            for dc in range(8):
                P.op("pe", MM(kps[:, 0:n], wqk[:, dc, 512:640], hT[:, dc, c0:c0 + n], start=(dc == 0), stop=(dc == 7)),
                     reads=rh, writes=[rkp])
            P.op("dve", CP(kbuf[:, 0:n], kps[:, 0:n]), reads=[rkp], writes=[rk])

        def proj_tok(lb, n, c0, bu=None, bg=None, bkv=None):
            ups, _, rups = psum(bu)
            gps, _, rgps = psum(bg)
            kvps, _, rkv = psum(bkv)
            for (ps_, rp_, a, b) in ((ups, rups, 0, 512), (gps, rgps, 512, 1024), (kvps, rkv, 1024, 1280)):
                for dc in range(8):
                    P.op("pe", MM(ps_[:n, 0:b - a], hT[:, dc, c0:c0 + n], wtok[:, dc, a:b], start=(dc == 0), stop=(dc == 7)),
                         reads=[RH[lb], R["wtok"]], writes=[rp_])
            return ups, rups, gps, rgps, kvps, rkv

        def gmlp_front(lb, n, ups, rups, gps, rgps, sample, bm=None):
            gelu2(ups[:n, :], rups, CB.u2f[:n, :], CB.R_u2)
            yield
            gelu2(gps[:n, :], rgps, CB.g2f[:n, :], CB.R_g2)
            yield
            sg, rsg = newstat()
            scr = CB.mraw[:n, 512:1024]
            P.op("act", ACT(scr, CB.g2f[:n, :], AF.Square), reads=[CB.R_g2], writes=[CB.R_gate2])
            P.op("dve", lambda e, sg=sg, scr=scr: e.tensor_reduce(out=sg[:n, 0:4], in_=scr.rearrange("p (g c) -> p g c", g=4),
                                                                  axis=mybir.AxisListType.X, op=ALU.add),
                 reads=[CB.R_gate2], writes=[rsg])
            rstd_from_ss(sg[:n, 0:4], sg[:n, 0:4], rsg, n, 1.0 / 128, EPS)
            dst = gn32 if sample else gnb
            rdst = R["otsb"] if sample else R["gnb"]
            bc = sg[:n, 0:4].unsqueeze(2).to_broadcast([n, 4, 128])
            P.op("dve", TT(CB.g2[:n, :, :], CB.g2[:n, :, :], bc, ALU.mult), reads=[CB.R_g2, rsg], writes=[CB.R_g2])
            P.op("dve", TT(dst[:n, :, :], CB.g2[:n, :, :], vnb[:n, :, :], ALU.mult), reads=[CB.R_g2, RC], writes=[rdst])
            if sample:
                P.op("dve", CP(gnb[:n, :, :], gn32[:n, :, :]), reads=[R["otsb"]], writes=[R["gnb"]])
                P.op("sp", DMA(scv_d, gn32[:n, :, :].rearrange("p a b -> p (a b)")), reads=[R["otsb"]], dma="o_scv")
            yield
            mps, _, rm = psum(bm)
            for G in range(4):
                if sample:
                    P.op("pe", MM(mps[:n, G * 128:(G + 1) * 128], BD[:n, G, :], gnb[:n, G, :]),
                         reads=[R["BD"], R["gnb"]], writes=[rm])
                else:
                    P.op("pe", MM(mps[:n, G * 128:(G + 1) * 128], WtT[:, G, :], gnb[:, G, :]),
                         reads=[R["WtT"], R["gnb"]], writes=[rm])
            bt = bsps if sample else bsp
            gdst = CB.mraw[:n, 512:1024].rearrange("p (g c) -> p g c", g=4)
            bcb = bt[:n, 0:4].unsqueeze(2).to_broadcast([n, 4, 128])
            P.op("dve", TT(gdst, mps[:n, :].rearrange("p (g c) -> p g c", g=4), bcb, ALU.add), reads=[rm, RC], writes=[CB.R_gate2])
            P.op("dve", TT(gdst, gdst, CB.u2[:n, :, :], ALU.mult), reads=[CB.R_gate2, CB.R_u2], writes=[CB.R_gate2])

        eb_ctr = [0]

        def exp_mask(sps, rsp, n, w, ebias_ap, dstPT, rdst, use_flag=False):
            i = eb_ctr[0] % 2
            eb_ctr[0] += 1
            eb_, re_ = ebuf[i], R[f"ebuf{i}"]
            P.op("act", ACT(eb_[:n, 0:w], sps[:n, 0:w], AF.Exp, scale=0.125), reads=[rsp], writes=[re_])
            if use_flag:
                P.op("dve", STT(dstPT[:n, 0:w], eb_[:n, 0:w], flag[:n, 0:1], ebias_ap[:n], ALU.mult, ALU.mult),
                     reads=[re_, RC], writes=[rdst])
            else:
                P.op("dve", TT(dstPT[:n, 0:w], eb_[:n, 0:w], ebias_ap[:n], ALU.mult), reads=[re_, RC], writes=[rdst])

        def mixer_frontA(gb, lb, S):
            n = 128
            c0 = lb * 128
            cur, prev = gb % 3, (gb - 1) % 3
            proj_qk(lb, n, c0, kT[cur], R[f"kT{cur}"], bq=0, bk=1)
            yield
            pi = 0
            pts = {}
            S["pts"] = pts
            for kvh in range(2):
                for kbi, kb in enumerate((prev, cur)):
                    sps, _, rsp = psum((0, 1, 6, 7)[kvh * 2 + kbi])
                    P.op("pe", MM(sps[:, :], kT[kb][:, :], CB.qTz[:, kvh, :], start=True, stop=False),
                         reads=[R[f"kT{kb}"], CB.R_qT], writes=[rsp])
                    if gb == 1 and kbi == 0:
                        btab = bmf[:, kvh * 4:(kvh + 1) * 4, :].rearrange("p a b -> p (a b)")
                    else:
                        btab = bm[:, kbi, kvh * 4:(kvh + 1) * 4, :].rearrange("p a b -> p (a b)")
                    P.op("pe", MM(sps[:, :], ident_b[:, :], btab, start=False, stop=True),
                         reads=[R["ident_b"], R["bm"]], writes=[rsp])
                    P.op("act", ACT(PT[pi][:, :], sps[:, :], AF.Exp, scale=0.125), reads=[rsp], writes=[R[f"PT{pi}"]])
                    pts[(kvh, kbi)] = pi
                    pi += 1
                yield

        def mixer_frontB(gb, lb, S):
            n = 128
            c0 = lb * 128
            cur = gb % 3
            ups, rups, gps, rgps, kvps, rkv = proj_tok(lb, n, c0, bu=2, bg=3, bkv=6)
            yield
            gm = gmlp_front(lb, n, ups, rups, gps, rgps, False, bm=6)
            S["gm"] = gm
            next(gm)
            yield
            next(gm)
            next(gm)
            P.op("act", ACT(Va[cur][:, :, 0:64], kvps[:, 128:256].rearrange("p (k d) -> p k d", k=2), AF.Copy),
                 reads=[rkv], writes=[R[f"Va{cur}"]])
            if gb == 16:
                P.op("act", ACT(kvout[:, :], kvps[:, 0:256], AF.Copy), reads=[rkv], writes=[R["kvout"]])
                P.op("sp", DMA(pk_d, kvout[:, 0:128]), reads=[R["kvout"]], dma="o_pk")
                P.op("sp", DMA(pv_d, kvout[:, 128:256]), reads=[R["kvout"]], dma="o_pv")
            yield

        def mixer_pv(gb, lb, S):
            n = 128
            cur, prev = gb % 3, (gb - 1) % 3
            pts = S["pts"]
            oA, _, rA = psum(4)
            oB, _, rB = psum(5)
            for kvh, (o, ro) in enumerate(((oA, rA), (oB, rB))):
                for g in range(4):
                    for kbi, kb in enumerate((prev, cur)):
                        pj = pts[(kvh, kbi)]
                        P.op("pe", MM(o[:, g * 65:(g + 1) * 65], PT[pj][:, g * 128:(g + 1) * 128], Va[kb][:, kvh, :],
                                      start=(kbi == 0), stop=(kbi == 1)),
                             reads=[R[f"PT{pj}"], R[f"Va{kb}"]], writes=[ro])
            S["tail"] = mixer_tail(lb, n, oA, rA, oB, rB, bpb=6, bops=((4, 5), (6, 7)))

        def mixer_halo(lb):
            n = 128
            c0 = lb * 128
            proj_qk(lb, n, c0, kT[0], R["kT0"], want_q=False)
            kvps, _, rkv = psum()
            for dc in range(8):
                P.op("pe", MM(kvps[:n, 0:128], hT[:, dc, c0:c0 + n], wtok[:, dc, 1152:1280], start=(dc == 0), stop=(dc == 7)),
                     reads=[RH[lb], R["wtok"]], writes=[rkv])
            P.op("act", ACT(Va[0][:, :, 0:64], kvps[:, 0:128].rearrange("p (k d) -> p k d", k=2), AF.Copy),
                 reads=[rkv], writes=[R["Va0"]])

        def mixer_sample_prep():
            allRA = [RA[j][s_] for j in range(6) for s_ in range(2)]
            P.op("pool", DMA(ckb, ck_d.rearrange("b r c -> r b c")), writes=[RF[0], RF[1]], dma="c_ck")
            P.op("pool", DMA(cvraw, cv_d.rearrange("b r c -> r b c")), writes=[RF[2], RF[3]], dma="c_cv")
            P.op("pool", MS(cvb_flat, 1.0), writes=allRA)
            P.op("dve", CP(cvb[:, :, :, 0:64], cvraw.rearrange("p b (k d) -> p b k d", k=2)),
                 reads=[RF[2], RF[3]], writes=allRA)
            for grp in range(2):
                _, pb, rb = psum()
                for bb in range(8):
                    b = grp * 8 + bb
                    P.op("pe", TR(pb[:, bb * 128:(bb + 1) * 128], ckb[:, b, :], ident_b[:, :]),
                         reads=[RF[0], RF[1], R["ident_b"]], writes=[rb])
                P.op("act", ACT(kTc[:, grp * 8:(grp + 1) * 8, :], pb[:, :].rearrange("p (a b) -> p a b", a=8), AF.Copy),
                     reads=[rb], writes=allRA)

        def mixer_sample(lb):
            n = 64
            c0 = lb * 128
            allRA = [RA[j][s_] for j in range(6) for s_ in range(2)]
            KS = os.environ.get("KSAMP", "")
            proj_qk(lb, n, c0, kT[2], R["kT2"], perm_q=True)
            ups, rups, gps, rgps, kvps, rkv = proj_tok(lb, n, c0)
            P.op("act", ACT(Va[2][:n, :, 0:64], kvps[:n, 128:256].rearrange("p (k d) -> p k d", k=2), AF.Copy),
                 reads=[rkv], writes=[R["Va2"]])
            P.op("act", ACT(kvout[:n, :], kvps[:n, 0:256], AF.Copy), reads=[rkv], writes=[R["kvout"]])
            for b in range(16):
                P.op("sp", DMA(sk_d[b, 124:128, :], kvout[b * 4:(b + 1) * 4, 0:128]), reads=[R["kvout"]], dma="o_skn")
                P.op("sp", DMA(sv_d[b, 124:128, :], kvout[b * 4:(b + 1) * 4, 128:256]), reads=[R["kvout"]], dma="o_svn")
            if KS == "s2":
                return
            for _ in gmlp_front(lb, n, ups, rups, gps, rgps, True):
                pass
            if KS == "s3":
                return
            for kvh in range(2):
                scps, _, rsc = psum()
                for b in range(16):
                    P.op("pe", MM(scps[:, b * 16:(b + 1) * 16], kTc[kvh * 64:(kvh + 1) * 64, b, :],
                                  CB.qT[kvh * 64:(kvh + 1) * 64, b * 16:(b + 1) * 16]),
                         reads=allRA + [CB.R_qT], writes=[rsc])
                exp_mask(scps, rsc, 128, 256, ebc[:, kvh * 256:(kvh + 1) * 256], PT[0][:, kvh * 256:(kvh + 1) * 256], R["PT0"])
            if KS == "s4":
                return
            for kvh in range(2):
                snps, _, rsn = psum()
                P.op("pe", MM(snps[:n, 0:256], kT[2][kvh * 64:(kvh + 1) * 64, 0:n], CB.qT[kvh * 64:(kvh + 1) * 64, 0:256]),
                     reads=[R["kT2"], CB.R_qT], writes=[rsn])
                exp_mask(snps, rsn, n, 256, ebn[:, kvh * 256:(kvh + 1) * 256], PT[1][:, kvh * 256:(kvh + 1) * 256], R["PT1"])
            otps, _, rot = psum()
            onps, _, ron = psum()
            for kvh in range(2):
                P.op("pe", MM(onps[0:65, kvh * 256:(kvh + 1) * 256], Va[2][:n, kvh, :], PT[1][:n, kvh * 256:(kvh + 1) * 256]),
                     reads=[R["Va2"], R["PT1"]], writes=[ron])
            for kvh in range(2):
                for b in range(16):
                    col = kvh * 256 + b * 16
                    P.op("pe", MM(otps[0:65, col:col + 16], cvb[:, b, kvh, :], PT[0][:, col:col + 16]),
                         reads=allRA + [R["PT0"]], writes=[rot])
            for kvh in range(2):
                src = otps[0:65, kvh * 256:(kvh + 1) * 256].rearrange("p (b g i) -> p g b i", b=16, g=4)
                dst = otsb[0:65, kvh * 256:(kvh + 1) * 256].rearrange("p (g b i) -> p g b i", g=4, b=16)
                P.op("act", ACT(dst, src, AF.Copy), reads=[rot], writes=[R["otsb"]])
            for kvh in range(2):
                dst = otsb[0:65, kvh * 256:(kvh + 1) * 256].rearrange("p (g b i) -> p g b i", g=4, b=16)
                src = onps[0:65, kvh * 256:(kvh + 1) * 256].rearrange("p (b g i) -> p g b i", b=16, g=4)
                P.op("dve", TT(dst, dst, src, ALU.add), reads=[R["otsb"], ron], writes=[R["otsb"]])
            if KS == "s5":
                return
            oA, _, rA = psum()
            oB, _, rB = psum()
            for kvh, (o, ro) in enumerate(((oA, rA), (oB, rB))):
                for g in range(4):
                    h = kvh * 4 + g
                    P.op("pe", TR(o[:n, g * 65:(g + 1) * 65], otsb[0:65, h * 64:(h + 1) * 64], ident_f[0:65, 0:65]),
                         reads=[R["otsb"], RC], writes=[ro])
            for _ in mixer_tail(lb, n, oA, rA, oB, rB):
                pass

        def final_norm(lb, n, row0):
            stt, rs = newstat()
            i = hb_ctr[0] % 2
            hb_ctr[0] += 1
            P.op("act", ACT(hb[i][:n, :], xt[:n, lb, :], AF.Square, accum_out=stt[:n, 0:1]),
                 reads=[RX[lb]], writes=[RHB[i], rs])
            rstd_from_ss(stt[:n, 0:1], stt[:n, 1:2], rs, n, 1.0 / D, EPS)
            P.op("dve", STT(xt[:n, lb, :], xt[:n, lb, :], stt[:n, 1:2], gains[:n, 3, :], ALU.mult, ALU.mult),
                 reads=[RX[lb], rs, RC], writes=[RX[lb]])
            P.op("sp", DMA(yout_d[row0:row0 + n, :], xt[:n, lb, :]), reads=[RX[lb]], dma=f"o_y{lb}")

        import os
        STOP = int(os.environ.get("KSTOP", "99"))
        for t in range(3):
            if STOP < 99 and t > 0:
                break
            if t >= int(os.environ.get("KTILES", "3")):
                break
            gblocks = list(range(t * 6, t * 6 + 6))
            ncols = {}
            for lb, gb in enumerate(gblocks):
                n = 64 if gb == 17 else 128
                ncols[lb] = n
                if t == 0 or STOP < 99:
                    P.op("sp", DMA(xt[:n, lb, :], xin_d[gb * 128:gb * 128 + n, :]), writes=[RX[lb]], dma=f"x{lb}")
            if STOP < 1:
                break
            if t == 0:
                part_hook.extend([late_init, late_init2])
            ffn(list(range(6)), ncols, 0, sub_major_first=True,
                after_first_norm=(lambda: ring_issue(NSLOT)) if t == 0 else ring_done)
            if t == 0:
                while bg_dma:
                    bg_dma.pop(0)()
                late_init_compute()
            if STOP < 2:
                break
            gens = []
            norm_phase(list(range(len(gblocks))), ncols, 1)
            if 17 in gblocks and STOP >= 4:
                mixer_sample_prep()
            for lb, gb in enumerate(gblocks):
                if STOP < 3 and gb > 0:
                    break
                if STOP < 4 and gb > 1:
                    break
                if gb == 0:
                    CB.set(0)
                    mixer_halo(lb)
                elif gb == 17:
                    pass
                else:
                    gens.append((gb, lb))

            def adv(par, g):
                CB.set(par)
                try:
                    next(g)
                    return True
                except StopIteration:
                    return False

            states = {gb: {} for gb, lb in gens}

            def step(g, par):
                CB.set(par)
                try:
                    next(g)
                except StopIteration:
                    pass

            fronts = {}

            def mk_front(i):
                gb_, lb_ = gens[i]
                fronts[i] = (mixer_frontA(gb_, lb_, states[gb_]), mixer_frontB(gb_, lb_, states[gb_]), gb_ % 2)

            if gens:
                mk_front(0)
                fa, fb, pz = fronts[0]
                for g_ in (fa, fa, fa, fb, fb, fb):
                    step(g_, pz)
            blocks2_ = [lb_ for lb_, gb_ in enumerate(gblocks) if gb_ != 0]
            subsA = subs_of(blocks2_, ncols)[0]
            grpA = [lb_ for lb_ in subsA[2] if lb_ * 128 >= subsA[0]]
            prenormed2 = []
            for i, (gb, lb) in enumerate(gens):
                if i == len(gens) - 1 and len(gens) >= 2 and STOP >= 99 and all(l_ < lb for l_ in grpA):
                    norm_phase(grpA, ncols, 2)
                    prenormed2 = list(grpA)
                Sb = states[gb]
                pb_ = gb % 2
                have_f = i + 1 < len(gens)
                if have_f:
                    mk_front(i + 1)
                    fa, fb, pf = fronts[i + 1]
                nop = iter(())
                if not have_f:
                    fa, fb, pf = nop, nop, 0
                step(fa, pf)
                CB.set(pb_)
                mixer_pv(gb, lb, Sb)
                step(Sb["tail"], pb_)
                step(Sb["gm"], pb_)
                step(Sb["tail"], pb_)
                step(fa, pf)
                step(fa, pf)
                step(Sb["tail"], pb_)
                step(fb, pf)
                step(fb, pf)
                step(fb, pf)
                step(Sb["tail"], pb_)
                step(Sb["tail"], pb_)
                step(Sb["tail"], pb_)
                step(Sb["tail"], pb_)
            if 17 in gblocks and STOP >= 4:
                CB.set(0)
                mixer_sample(gblocks.index(17))
            if STOP < 5:
                break
            blocks2 = [lb for lb, gb in enumerate(gblocks) if gb != 0]

            def block_final(lb, t=t, gblocks=gblocks, ncols=ncols):
                n = ncols[lb]
                row0 = (gblocks[lb] - 1) * 128
                stt, rs = newstat()
                j = junk_ctr[0] % 2
                junk_ctr[0] += 1
                P.op("act", ACT(hb[j][:n, :], xt[:n, lb, :], AF.Square, accum_out=stt[:n, 0:1]),
                     reads=[RX[lb]], writes=[RHB[j], rs])
                rstd_from_ss(stt[:n, 0:1], stt[:n, 1:2], rs, n, 1.0 / D, EPS)
                P.op("dve", STT(xt[:n, lb, :], xt[:n, lb, :], stt[:n, 1:2], gains[:n, 3, :], ALU.mult, ALU.mult),
                     reads=[RX[lb], rs, RC], writes=[RX[lb]])
                P.op("sp", DMA(yout_d[row0:row0 + n, :], xt[:n, lb, :]), reads=[RX[lb]], dma=f"o_y{lb}")
                if t < 2 and STOP >= 99:
                    pend_loads.append(lb)
                    if t == 0 and lb == 1:
                        pend_loads.insert(0, 0)
                    while len(pend_loads) > 1:
                        emit_load(pend_loads.pop(0))

            def emit_load(l2, t=t):
                gb2 = (t + 1) * 6 + l2
                n2 = 64 if gb2 == 17 else 128
                P.op("sp", DMA(xt[:n2, l2, :], xin_d[gb2 * 128:gb2 * 128 + n2, :]), writes=[RX[l2]], dma=f"x{l2}")

            pend_loads = []
            if t == 2:
                part_hook.extend([lambda: None, late_copies])
            ffn(blocks2, ncols, 2, on_block_final=block_final, skip_last_ring_done=(t < 2 and STOP >= 99), prenormed=prenormed2,
                sub_major_first=True)
            while pend_loads:
                emit_load(pend_loads.pop(0))
            if STOP < 6:
                break
        if STOP < 99:
            for lb in range(1, 6):
                P.op("sp", DMA(yout_d[(lb - 1) * 128:lb * 128, :], xt[:, lb, :]), reads=[RX[lb]], dma=f"o_y{lb}")
        P.emit(nc, st)
    return nc


_SLOPES = (2.0 ** (-8.0 * np.arange(1, 9, dtype=np.float32) / 8)).astype(np.float32)


def _const_tables():
    j = np.arange(128)[:, None]
    i = np.arange(128)[None, :]
    ebm = np.zeros((128, 2, 8, 128), np.float32)
    NEG = -30000.0
    for h in range(8):
        s = np.float64(_SLOPES[h])
        d_prev = 128 + i - j
        ebm[:, 0, h, :] = np.where(j > i, -8.0 * s * d_prev, NEG)
        d_cur = i - j
        ebm[:, 1, h, :] = np.where(j <= i, -8.0 * s * d_cur, NEG)
    r = np.arange(128)[:, None]
    ii = np.arange(4)[None, :]
    ebc = np.zeros((128, 2, 16, 4, 4), np.float32)
    for h in range(8):
        s = np.float64(_SLOPES[h])
        v = np.where(r >= ii + 1, np.exp(-s * (128 + ii - r)), 0.0)
        ebc[:, h // 4, :, h % 4, :] = v[:, None, :]
    ebn = np.zeros((16, 4, 2, 16, 4, 4), np.float32)
    jj = np.arange(4)[:, None]
    i4 = np.arange(4)[None, :]
    for h in range(8):
        s = np.float64(_SLOPES[h])
        v = np.where(jj <= i4, np.exp(-s * (i4 - jj)), 0.0)
        for b in range(16):
            ebn[b, :, h // 4, b, h % 4, :] = v
    tril = (j <= i).astype(np.float32)
    masks = np.zeros((16, 4, 16, 4), np.float32)
    for b in range(16):
        masks[b, :, b, :] = (jj <= i4)
    return (ebm.reshape(128, -1), ebc.reshape(128, 512), ebn.reshape(64, 512), tril, masks.reshape(64, 64))


def _ffn_pieces(wg, wu, wd):
    wgp = np.ascontiguousarray(wg.reshape(8, 128, NJ, 128).transpose(2, 1, 0, 3)).reshape(NJ, 128, 1024)
    wup = np.ascontiguousarray(wu.reshape(8, 128, NJ, 128).transpose(2, 1, 0, 3)).reshape(NJ, 128, 1024)
    wdp = wd.reshape(NJ, 128, 1024)
    out = []
    for (a, b) in PARTS:
        for j in range(a, b):
            out.append(wgp[j])
            out.append(wup[j])
        for j in range(a, b):
            out.append(wdp[j])
    return out


_NC_CACHE = {}


def kernel(x_prompt, x_sample, cache_k, cache_v, ffn1_norm, ffn1_w_gate, ffn1_w_up, ffn1_w_down,
           mix_norm, w_in, attn_sinks, gmlp_v_norm, gmlp_w_spatial, gmlp_b_spatial,
           attn_out_norm, gmlp_out_norm, w_out, ffn2_norm, ffn2_w_gate, ffn2_w_up, ffn2_w_down,
           final_norm):
    f = lambda a: np.asarray(a, dtype=np.float32)
    x_prompt, x_sample, cache_k, cache_v = f(x_prompt), f(x_sample), f(cache_k), f(cache_v)
    w_in0 = f(w_in)[0]
    ebm, ebc, ebn, tril, masks = _const_tables()

    wffn = np.stack(_ffn_pieces(f(ffn1_w_gate)[0], f(ffn1_w_up)[0], f(ffn1_w_down)[0]) +
                    _ffn_pieces(f(ffn2_w_gate)[0], f(ffn2_w_up)[0], f(ffn2_w_down)[0]), axis=0)
    qperm = np.concatenate([np.concatenate([np.arange(j * 64, (j + 1) * 64), np.arange((4 + j) * 64, (5 + j) * 64)])
                            for j in range(4)])
    cols_qk = np.concatenate([qperm, np.arange(512, 640)])
    cols_tok = np.concatenate([np.arange(768, 1280), np.arange(1280, 1792), np.arange(512, 768)])

    def lay(wmat):
        C = wmat.shape[1]
        return np.ascontiguousarray(wmat.reshape(8, 128, C).transpose(1, 0, 2)).reshape(128, 8 * C)

    wqk = lay(w_in0[:, cols_qk])
    wtok = lay(w_in0[:, cols_tok])
    wout = lay(f(w_out)[0])
    gout = np.concatenate([f(attn_out_norm)[0], f(gmlp_out_norm)[0]])
    gains = np.stack([np.broadcast_to(g, (128, D)) for g in
                      (f(ffn1_norm)[0], f(mix_norm)[0], f(ffn2_norm)[0], f(final_norm), gout)], axis=0)
    gains = np.ascontiguousarray(gains)
    vnb = np.ascontiguousarray(np.broadcast_to(f(gmlp_v_norm)[0].reshape(1, 512), (128, 512)))
    wsp_full = f(gmlp_w_spatial)[0]
    wsp = np.ascontiguousarray(wsp_full.transpose(2, 0, 1)).reshape(128, 512)
    w4 = wsp_full[:, 0:4, 0:4]
    wsps = np.ascontiguousarray(np.broadcast_to(w4.transpose(2, 0, 1)[None, :, :, None, :], (16, 4, 4, 16, 4))).reshape(64, 256)
    bsp = np.ascontiguousarray(f(gmlp_b_spatial)[0].T)
    bsps = np.ascontiguousarray(np.tile(bsp[0:4], (16, 1)))
    sinks = np.ascontiguousarray(np.broadcast_to(f(attn_sinks)[0].reshape(1, 8), (128, 8)))
    ident = np.eye(128, dtype=np.float32)

    bm_prev = np.ascontiguousarray(ebm.reshape(128, 2, 1024)[:, 0, :])
    bm_none = np.full((128, 1024), -30000.0, np.float32)
    in_maps = []
    for c in range(NCORES):
        b, q = c // 4, c % 4
        main = x_prompt[b, q * 2048:(q + 1) * 2048]
        halo = x_prompt[b, q * 2048 - 128:q * 2048] if q > 0 else np.zeros((128, D), np.float32)
        samp = x_sample[c * 16:(c + 1) * 16].reshape(64, D)
        xin = np.concatenate([halo, main, samp], axis=0)
        in_maps.append({
            "xin": np.ascontiguousarray(xin), "wffn": wffn, "wqk": wqk, "wtok": wtok, "wout": wout,
            "gains": gains, "vnb": vnb, "wsp": wsp, "tril": tril, "wsps": wsps, "masks": masks,
            "bsp": bsp, "bsps": bsps, "sinks": sinks,
            "flag": np.full((128, 1), 1.0 if q > 0 else 0.0, np.float32),
            "bm": ebm, "bmf": (bm_prev if q > 0 else bm_none), "ebc": ebc, "ebn": ebn, "ident": ident,
            "ck": np.ascontiguousarray(cache_k[0, c * 16:(c + 1) * 16].reshape(16, 128, 128)),
            "cv": np.ascontiguousarray(cache_v[0, c * 16:(c + 1) * 16].reshape(16, 128, 128)),
        })
    if "nc" not in _NC_CACHE:
        _NC_CACHE["nc"] = build_program()
    nc = _NC_CACHE["nc"]
    ncr = int(os.environ.get("KCORES", NCORES))
    res = run_bass_kernel_spmd(nc, in_maps[:ncr], core_ids=list(range(ncr)))
    rs = list(res.results)
    while len(rs) < NCORES:
        rs.append(rs[0])
    y_prompt = np.stack([np.concatenate([rs[b * 4 + q]["yout"][0:2048] for q in range(4)], axis=0) for b in range(2)], axis=0)
    y_sample = np.concatenate([rs[c]["yout"][2048:2112].reshape(16, 4, D) for c in range(NCORES)], axis=0)
    prompt_k = np.stack([rs[b * 4 + 3]["pk"].reshape(128, 2, 64) for b in range(2)], axis=0)[None]
    prompt_v = np.stack([rs[b * 4 + 3]["pv"].reshape(128, 2, 64) for b in range(2)], axis=0)[None]
    sample_k = np.concatenate([rs[c]["sk"].reshape(16, 128, 2, 64) for c in range(NCORES)], axis=0)[None]
    sample_v = np.concatenate([rs[c]["sv"].reshape(16, 128, 2, 64) for c in range(NCORES)], axis=0)[None]
    scv = np.concatenate([rs[c]["scv"].reshape(16, 4, 512) for c in range(NCORES)], axis=0)[None]
    return (y_prompt.astype(np.float32), y_sample.astype(np.float32), prompt_k.astype(np.float32),
            prompt_v.astype(np.float32), sample_k.astype(np.float32), sample_v.astype(np.float32),
            scv.astype(np.float32))
```

```python
import os
import numpy as np
import concourse.bass as bass
import concourse.mybir as mybir
from concourse.bass_utils import run_bass_kernel_spmd
from contextlib import ExitStack

F32 = mybir.dt.float32
BF16 = mybir.dt.bfloat16
AF = mybir.ActivationFunctionType
ALU = mybir.AluOpType

D = 1024
DFF = 2816
NJ = 22
NSLOT = 14
PARTS = [(0, 6), (6, 11), (11, 17), (17, 22)]
EPS = 1e-6
NCORES = 8
PIECES_PER_FFN = 66
GELU_C = 0.7978845608028654


class Res:
    __slots__ = ("name", "w", "r", "rd")

    def __init__(self, name=""):
        self.name = name
        self.w = None
        self.r = {}
        self.rd = []


class Op:
    __slots__ = ("eng", "fn", "deps", "sem", "signal", "val")

    def __init__(self, eng, fn, deps, sem=None):
        self.eng = eng
        self.fn = fn
        self.deps = deps
        self.sem = sem
        self.signal = sem is not None
        self.val = None


class Prog:
    ENGS = ("pe", "act", "dve", "pool", "sp")

    def __init__(self):
        self.lists = {e: [] for e in self.ENGS}

    def op(self, eng, fn, reads=(), writes=(), dma=None):
        deps = []
        for r in reads:
            if r.w is not None:
                deps.append(r.w)
        for w in writes:
            if w.w is not None:
                deps.append(w.w)
            deps.extend(w.r.values())
            deps.extend(w.rd)
        o = Op(eng, fn, deps, sem=dma)
        for r in reads:
            if dma is None:
                r.r[eng] = o
            else:
                r.rd.append(o)
        for w in writes:
            w.w = o
            w.r = {}
            w.rd = []
        for d in deps:
            if d.sem is None and not (d.eng == "pe" and eng == "pe" and dma is None):
                d.signal = True
        self.lists[eng].append(o)
        return o

    def emit(self, nc, stack):
        eng_sem = {e: stack.enter_context(nc.semaphore("s_" + e)) for e in self.ENGS}
        dsem = {}
        dcount = {}
        for e in self.ENGS:
            cnt = 0
            for o in self.lists[e]:
                if o.sem is None:
                    if o.signal:
                        cnt += 1
                        o.val = cnt
                else:
                    if o.sem not in dsem:
                        dsem[o.sem] = stack.enter_context(nc.semaphore("d_" + o.sem))
                        dcount[o.sem] = 0
                    dcount[o.sem] += 16
                    o.val = dcount[o.sem]
        block = stack.enter_context(nc.Block())
        lists = self.lists

        def run(e, eng):
            waited = {}
            for o in lists[e]:
                need = {}
                for d in o.deps:
                    if d.sem is None:
                        if d.eng == "pe" and e == "pe" and o.sem is None:
                            continue
                        key = ("e", d.eng)
                        s = eng_sem[d.eng]
                    else:
                        key = ("d", d.sem)
                        s = dsem[d.sem]
                    v = d.val
                    if waited.get(key, 0) >= v:
                        continue
                    if key not in need or need[key][1] < v:
                        need[key] = (s, v)
                for key, (s, v) in need.items():
                    eng.wait_ge(s, v)
                    waited[key] = v
                ins = o.fn(eng)
                if o.signal:
                    if o.sem is None:
                        ins.then_inc(eng_sem[e], 1)
                    else:
                        ins.then_inc(dsem[o.sem], 16)
            if e == "sp":
                for name, s in dsem.items():
                    eng.wait_ge(s, dcount[name])

        @block.tensor
        def _(eng):
            run("pe", eng)

        @block.scalar
        def _(eng):
            run("act", eng)

        @block.vector
        def _(eng):
            run("dve", eng)

        @block.gpsimd
        def _(eng):
            run("pool", eng)

        @block.sync
        def _(eng):
            run("sp", eng)


def MM(out, lhsT, rhs, start=True, stop=True):
    return lambda e: e.matmul(out, lhsT=lhsT, rhs=rhs, start=start, stop=stop)


def TR(out, in_, ident):
    return lambda e: e.transpose(out, in_, ident)


def ACT(out, in_, func, **kw):
    return lambda e: e.activation(out=out, in_=in_, func=func, **kw)


def TT(out, in0, in1, op):
    return lambda e: e.tensor_tensor(out=out, in0=in0, in1=in1, op=op)


def STT(out, in0, scalar, in1, op0, op1):
    return lambda e: e.scalar_tensor_tensor(out=out, in0=in0, scalar=scalar, in1=in1, op0=op0, op1=op1)


def TS(out, in0, s1, s2, op0, op1):
    return lambda e: e.tensor_scalar(out=out, in0=in0, scalar1=s1, scalar2=s2, op0=op0, op1=op1)


def CP(out, in_):
    return lambda e: e.tensor_copy(out=out, in_=in_)


def RCP(out, in_):
    return lambda e: e.reciprocal(out=out, in_=in_)


def MS(ap, v):
    return lambda e: e.memset(ap, v)


def DMA(out, in_):
    return lambda e: e.dma_start(out=out, in_=in_)

def build_program():
    nc = bass.Bass("TRN2", target_bir_lowering=False)

    def din(name, shape):
        return nc.dram_tensor(name, list(shape), F32, kind="ExternalInput").ap()

    def dout(name, shape):
        return nc.dram_tensor(name, list(shape), F32, kind="ExternalOutput").ap()

    xin_d = din("xin", [2240, D])
    wffn_d = din("wffn", [2 * PIECES_PER_FFN, 128, 1024])
    wqk_d = din("wqk", [128, 8 * 640])
    wtok_d = din("wtok", [128, 8 * 1280])
    wout_d = din("wout", [128, 8 * 1024])
    gains_d = din("gains", [5, 128, 1024])
    vnb_d = din("vnb", [128, 512])
    wsp_d = din("wsp", [128, 512])
    tril_d = din("tril", [128, 128])
    wsps_d = din("wsps", [64, 256])
    masks_d = din("masks", [64, 64])
    bsp_d = din("bsp", [128, 4])
    bsps_d = din("bsps", [64, 4])
    sinks_d = din("sinks", [128, 8])
    flag_d = din("flag", [128, 1])
    bm_d = din("bm", [128, 2 * 8 * 128])
    bmf_d = din("bmf", [128, 8 * 128])
    ebc_d = din("ebc", [128, 512])
    ebn_d = din("ebn", [64, 512])
    ident_d = din("ident", [128, 128])
    ck_d = din("ck", [16, 128, 128])
    cv_d = din("cv", [16, 128, 128])

    yout_d = dout("yout", [2112, D])
    pk_d = dout("pk", [128, 128])
    pv_d = dout("pv", [128, 128])
    sk_d = dout("sk", [16, 128, 128])
    sv_d = dout("sv", [16, 128, 128])
    scv_d = dout("scv", [64, 512])

    P = Prog()
    with ExitStack() as st:
        def sb(name, shape, dt):
            return st.enter_context(nc.sbuf_tensor("sb_" + name, list(shape), dt))

        xt = sb("xt", [128, 6, D], F32)
        hT = sb("hT", [128, 8, 768], BF16)
        aT = sb("aT", [128, 6, 768], BF16)
        ring = [sb(f"ring{i}", [128, 1024], BF16) for i in range(NSLOT)]
        wqk = sb("wqk", [128, 8, 640], BF16)
        wtok = sb("wtok", [128, 8, 1280], BF16)
        wout = sb("wout", [128, 8, 1024], BF16)
        bm = sb("bm", [128, 2, 8, 128], BF16)
        bmf = sb("bmf", [128, 8, 128], BF16)
        ebc = sb("ebc", [128, 512], F32)
        ebn = sb("ebn", [64, 512], F32)
        gains = sb("gains", [128, 5, D], F32)
        vnb = sb("vnb", [128, 4, 128], F32)
        hb = [sb(f"hb{i}", [128, D], BF16) for i in range(2)]
        ftmp = sb("ftmp", [128, 4, 512], F32)
        gnb = sb("gnb", [128, 4, 128], BF16)
        ebuf = [sb(f"ebuf{i}", [128, 512], F32) for i in range(2)]
        PT = [sb(f"PT{i}", [128, 512], BF16) for i in range(4)]
        class BufSet:
            pass

        sets = []
        for pz in range(2):
            bs = BufSet()
            bs.u2 = sb(f"u2_{pz}", [128, 4, 128], F32)
            bs.g2 = sb(f"g2_{pz}", [128, 4, 128], F32)
            bs.u2f = bs.u2[:, :, :].rearrange("p a b -> p (a b)")
            bs.g2f = bs.g2[:, :, :].rearrange("p a b -> p (a b)")
            bs.mraw = sb(f"mraw_{pz}", [128, D], F32)
            bs.m = sb(f"m_{pz}", [128, D], BF16)
            bs.mT = sb(f"mT_{pz}", [128, 8, 128], BF16)
            bs.qTz = sb(f"qTz_{pz}", [128, 2, 512], BF16)
            bs.qT = bs.qTz[:, 0, :]
            for k in ["u2", "g2", "attn", "gate2", "m_a", "m_g", "mT", "qT"]:
                setattr(bs, "R_" + k, Res(f"{k}_{pz}"))
            sets.append(bs)

        class _CB:
            def set(self, pz):
                self.__dict__.update(sets[pz].__dict__)

        CB = _CB()
        CB.set(0)
        kT = [sb(f"kT{i}", [128, 128], BF16) for i in range(3)]
        Va = [sb(f"Va{i}", [128, 2, 65], BF16) for i in range(3)]
        kvout = sb("kvout", [128, 256], F32)
        otsb = sb("otsb", [128, 512], F32)
        gn32 = otsb[:, :].rearrange("p (a b) -> p a b", a=4)
        WtT = sb("WtT", [128, 4, 128], BF16)
        BD = sb("BD", [64, 4, 64], BF16)
        ident_f = sb("ident_f", [128, 128], F32)
        ident_b = sb("ident_b", [128, 128], BF16)
        stat = sb("stat", [128, 80], F32)
        nst = sb("nst", [128, 4, 16], F32)
        mh = sb("mh", [128, 16], F32)
        bsp = sb("bsp", [128, 4], F32)
        bsps = sb("bsps", [64, 4], F32)
        esink = sb("esink", [128, 8], F32)
        flag = sb("flag", [128, 1], F32)

        aT_flat = aT[:, :, :].rearrange("p a b -> p (a b)")
        kTc = aT_flat[:, 0:2048].rearrange("p (b r) -> p b r", b=16)
        cvb_flat = aT_flat[:, 2048:2048 + 2080]
        cvb = cvb_flat.rearrange("p (b k d) -> p b k d", b=16, k=2)
        ftmp_b = ftmp.bitcast(BF16)
        ckb = ftmp_b[:, 0:2, :].rearrange("p a b -> p (a b)").rearrange("p (b c) -> p b c", b=16)
        cvraw = ftmp_b[:, 2:4, :].rearrange("p a b -> p (a b)").rearrange("p (b c) -> p b c", b=16)

        banks = [st.enter_context(nc.psum_tensor(f"ps{i}", [128, 512], F32)) for i in range(8)]
        banks_b = [b.bitcast(BF16) for b in banks]
        RB = [Res(f"bank{i}") for i in range(8)]
        bank_ctr = [0]

        def psum(i=None):
            if i is None:
                i = bank_ctr[0] % 8
                bank_ctr[0] += 1
            return banks[i], banks_b[i], RB[i]

        RX = [Res(f"x{i}") for i in range(6)]
        RH = [Res(f"h{i}") for i in range(6)]
        RA = [[Res(f"a{j}_{s}") for s in range(2)] for j in range(6)]
        RRING = [Res(f"ring{i}") for i in range(NSLOT)]
        RHB = [Res("hb0"), Res("hb1")]
        RF = [Res(f"ftmp{i}") for i in range(4)]
        RC = Res("consts")
        R = {k: Res(k) for k in ["u2", "g2", "gnb", "gn32", "attn", "gate2", "m_a", "m_g", "mT", "ebuf0", "ebuf1",
                                 "PT0", "PT1", "PT2", "PT3", "qT", "kT0", "kT1", "kT2", "Va0", "Va1", "Va2",
                                 "kvout", "otsb", "WtT", "BD", "ident_b", "esink", "mh", "wqk", "wtok", "wout",
                                 "dn", "bm"]}
        RSTAT = [Res(f"stat{i}") for i in range(16)]
        stat_ctr = [0]
        dn8 = stat[:, 64:72]

        def newstat():
            i = stat_ctr[0] % 16
            stat_ctr[0] += 1
            return stat[:, i * 4:(i + 1) * 4], RSTAT[i]

        P.op("sp", DMA(ident_f[:, :], ident_d), writes=[RC], dma="c_ident")
        P.op("sp", DMA(gains[:, 0, :], gains_d[0]), writes=[RC], dma="c_gains0")

        def late_init():
            P.op("sp", DMA(gains[:, 1:5, :], gains_d[1:5].rearrange("g p d -> p g d")), writes=[RC], dma="c_gains")
            P.op("sp", DMA(vnb[:, :, :], vnb_d.rearrange("p (g c) -> p g c", g=4)), writes=[RC], dma="c_vnb")
            P.op("pool", DMA(bm[:, :, :, :], bm_d.rearrange("p (a h q) -> p a h q", a=2, h=8)), writes=[R["bm"]], dma="c_bm")
            P.op("pool", DMA(bmf[:, :, :], bmf_d.rearrange("p (h q) -> p h q", h=8)), writes=[R["bm"]], dma="c_bmf")
            P.op("sp", DMA(ebc[:, :], ebc_d), writes=[RC], dma="c_ebc")
            P.op("sp", DMA(ebn[:, :], ebn_d), writes=[RC], dma="c_ebn")
            P.op("sp", DMA(bsp[:, :], bsp_d), writes=[RC], dma="c_bsp")
            P.op("sp", DMA(bsps[:, :], bsps_d), writes=[RC], dma="c_bsps")
            P.op("sp", DMA(esink[:, :], sinks_d), writes=[R["esink"]], dma="c_sinks")
            P.op("sp", DMA(flag[:, :], flag_d), writes=[RC], dma="c_flag")
            stg = sets[1].mraw
            rstg = sets[1].R_attn
            P.op("sp", DMA(stg[:, 0:512], wsp_d), writes=[rstg], dma="c_wsp")
            P.op("sp", DMA(stg[:, 512:640], tril_d), writes=[rstg], dma="c_tril")
            P.op("sp", DMA(stg[0:64, 640:896], wsps_d), writes=[rstg], dma="c_wsps")
            P.op("sp", DMA(stg[0:64, 896:960], masks_d), writes=[rstg], dma="c_masks")
            wqk_v = wqk_d.rearrange("p (a b) -> p a b", a=8)
            wtok_v = wtok_d.rearrange("p (a b) -> p a b", a=8)
            wout_v = wout_d.rearrange("p (a b) -> p a b", a=8)
            for dc in range(0, 8, 2):
                bg_dma.append(lambda dc=dc: P.op("pool", DMA(wtok[:, dc:dc + 2, :], wtok_v[:, dc:dc + 2, :]), writes=[R["wtok"]], dma="c_wtok"))
            for dc in range(0, 8, 4):
                bg_dma.append(lambda dc=dc: P.op("pool", DMA(wqk[:, dc:dc + 4, :], wqk_v[:, dc:dc + 4, :]), writes=[R["wqk"]], dma="c_wqk"))
            for dc in range(0, 8, 2):
                bg_dma.append(lambda dc=dc: P.op("pool", DMA(wout[:, dc:dc + 2, :], wout_v[:, dc:dc + 2, :]), writes=[R["wout"]], dma="c_wout"))

        def late_init2():
            pass

        def late_copies():
            P.op("pool", DMA(sk_d[:, 0:124, :], ck_d[:, 4:128, :]), dma="o_skc")
            P.op("pool", DMA(sv_d[:, 0:124, :], cv_d[:, 4:128, :]), dma="o_svc")

        def late_init_compute():
            stg = sets[1].mraw
            rstg = sets[1].R_attn
            P.op("act", ACT(esink[:, :], esink[:, :], AF.Exp), reads=[R["esink"]], writes=[R["esink"]])
            for G in range(4):
                P.op("dve", TT(WtT[:, G, :], stg[:, G * 128:(G + 1) * 128], stg[:, 512:640], ALU.mult),
                     reads=[rstg], writes=[R["WtT"]])
                P.op("dve", TT(BD[:, G, :], stg[0:64, 640 + G * 64:640 + (G + 1) * 64], stg[0:64, 896:960], ALU.mult),
                     reads=[rstg], writes=[R["BD"]])

        for pz in range(2):
            P.op("pool", MS(sets[pz].qTz[:, :, :], 0.0), writes=[sets[pz].R_qT])
        P.op("pool", MS(mh[:, 0:8], -0.5), writes=[R["mh"]])
        P.op("pool", MS(mh[:, 8:9], EPS), writes=[R["mh"]])
        P.op("pool", MS(mh[:, 9:10], EPS), writes=[R["mh"]])
        epsv = mh[:, 8:10]
        for i in range(3):
            P.op("pool", MS(Va[i][:, :, :], 1.0), writes=[R[f"Va{i}"]])
        P.op("dve", CP(ident_b[:, :], ident_f[:, :]), reads=[RC], writes=[R["ident_b"]])
        TOTAL_PIECES = 3 * 2 * PIECES_PER_FFN
        ring_state = {"issued": 0, "next": 0}

        def ring_issue(upto):
            while ring_state["issued"] < min(upto, TOTAL_PIECES):
                k = ring_state["issued"]
                s = k % NSLOT
                extra = list(RX) if (4 <= k < NSLOT) else []
                P.op("pool", DMA(ring[s][:, :], wffn_d[k % (2 * PIECES_PER_FFN)]), reads=extra, writes=[RRING[s]], dma=f"ring{s}")
                ring_state["issued"] += 1

        def ring_get():
            k = ring_state["next"]
            ring_state["next"] += 1
            assert k < ring_state["issued"], "ring underflow"
            s = k % NSLOT
            return ring[s], RRING[s]

        bg_dma = []

        bg_ctr = [0]

        def ring_done():
            ring_issue(ring_state["next"] + NSLOT)
            bg_ctr[0] += 1
            if bg_dma and bg_ctr[0] % 2 == 0:
                bg_dma.pop(0)()

        ring_issue(4)

        def rstd_from_ss(ss_ap, out_ap, res, n, inv_count, eps):
            w = out_ap.shape[1]
            P.op("dve", TS(out_ap, ss_ap, inv_count, eps, ALU.mult, ALU.add), reads=[res], writes=[res])
            P.op("pool", TT(out_ap, out_ap, mh[:n, 0:w], ALU.pow), reads=[res, R["mh"]], writes=[res])

        hb_ctr = [0]
        RNST = [Res(f"nst{i}") for i in range(4)]
        nst_ctr = [0]
        junk_ctr = [0]

        P.op("pool", MS(nst[:, :, :], 1.0), writes=RNST)

        def norm_stats(blocks, ncols):
            i = nst_ctr[0] % 4
            nst_ctr[0] += 1
            t, rt = nst[:, i, :], RNST[i]
            for k, lb in enumerate(blocks):
                n = ncols[lb]
                j = junk_ctr[0] % 2
                junk_ctr[0] += 1
                jk = ftmp_b[:, 2 + j, :]
                P.op("act", ACT(jk[:n, :], xt[:n, lb, :], AF.Square, accum_out=t[:n, k:k + 1]),
                     reads=[RX[lb]], writes=[RF[2 + j], rt])
            nb = len(blocks)
            P.op("dve", TS(t[:, 8:8 + nb], t[:, 0:nb], 1.0 / D, EPS, ALU.mult, ALU.add), reads=[rt], writes=[rt])
            P.op("pool", TT(t[:, 8:8 + nb], t[:, 8:8 + nb], mh[:, 0:nb], ALU.pow),
                 reads=[rt, R["mh"]], writes=[rt])
            return t, rt

        def norm_phase(blocks, ncols, gidx):
            t, rt = norm_stats(blocks, ncols)
            for k, lb in enumerate(blocks):
                n = ncols[lb]
                c0 = lb * 128
                i = hb_ctr[0] % 2
                hb_ctr[0] += 1
                hbt, rhb = hb[i], RHB[i]
                P.op("dve", STT(hbt[:n, :], xt[:n, lb, :], t[:n, 8 + k:9 + k], gains[:n, gidx, :], ALU.mult, ALU.mult),
                     reads=[RX[lb], rt, RC], writes=[rhb])
                _, pb, rb = psum()
                for dc in range(8):
                    P.op("pe", TR(pb[:, dc * n:(dc + 1) * n], hbt[:n, dc * 128:(dc + 1) * 128], ident_b[:n, :n]),
                         reads=[rhb, R["ident_b"]], writes=[rb])
                P.op("act", ACT(hT[:, :, c0:c0 + n], pb[:, 0:8 * n].rearrange("p (a b) -> p a b", a=8), AF.Copy),
                     reads=[rb], writes=[RH[lb]])

        def norm_transpose(lb, n, gidx, c0):
            i = hb_ctr[0] % 2
            hb_ctr[0] += 1
            hbt, rhb = hb[i], RHB[i]
            stt, rs = newstat()
            P.op("act", ACT(hbt[:n, :], xt[:n, lb, :], AF.Square, accum_out=stt[:n, 0:1]),
                 reads=[RX[lb]], writes=[rhb, rs])
            rstd_from_ss(stt[:n, 0:1], stt[:n, 1:2], rs, n, 1.0 / D, EPS)
            P.op("dve", STT(hbt[:n, :], xt[:n, lb, :], stt[:n, 1:2], gains[:n, gidx, :], ALU.mult, ALU.mult),
                 reads=[RX[lb], rs, RC], writes=[rhb])
            _, pb, rb = psum()
            for dc in range(8):
                P.op("pe", TR(pb[:, dc * n:(dc + 1) * n], hbt[:n, dc * 128:(dc + 1) * 128], ident_b[:n, :n]),
                     reads=[rhb, R["ident_b"]], writes=[rb])
            P.op("act", ACT(hT[:, :, c0:c0 + n], pb[:, 0:8 * n].rearrange("p (a b) -> p a b", a=8), AF.Copy),
                 reads=[rb], writes=[RH[lb]])

        def subs_of(blocks, ncols):
            c_start = blocks[0] * 128
            c_end = blocks[-1] * 128 + ncols[blocks[-1]]
            out = []
            c = c_start
            while c < c_end:
                ce = min(c + 512, c_end)
                lbs = [lb for lb in blocks if lb * 128 < ce and lb * 128 + ncols[lb] > c]
                out.append((c, ce, lbs))
                c = ce
            return out

        ft_ctr = [0]

        part_hook = []

        def ffn(blocks, ncols, gidx, on_block_final=None, sub_major_first=False, after_first_norm=None, skip_last_ring_done=False,
                prenormed=()):
            pend_final = []
            for (c0_, c1_, lbs_) in subs_of(blocks, ncols):
                grp = [lb for lb in lbs_ if lb * 128 >= c0_ and lb not in prenormed]
                if grp:
                    norm_phase(grp, ncols, gidx)
                if after_first_norm is not None:
                    after_first_norm()
                    after_first_norm = None
            KFFN = os.environ.get("KFFN", "")
            if KFFN == "n":
                return
            subs = subs_of(blocks, ncols)
            sub_of_lb = {}
            for si, (c0, c1, lbs) in enumerate(subs):
                for lb in lbs:
                    sub_of_lb.setdefault(lb, []).append(si)
            for (ja, jb) in PARTS:
                npart = jb - ja
                def gate_up(jj, si, wg, rwg, wu, rwu):
                    c0, c1, lbs = subs[si]
                    w = c1 - c0
                    gps, _, rg = psum()
                    ups, _, ru = psum()
                    rh = [RH[lb] for lb in lbs]
                    for dc in range(8):
                        P.op("pe", MM(gps[:, 0:w], wg[:, dc * 128:(dc + 1) * 128], hT[:, dc, c0:c1],
                                      start=(dc == 0), stop=(dc == 7)), reads=[rwg] + rh, writes=[rg])
                    for dc in range(8):
                        P.op("pe", MM(ups[:, 0:w], wu[:, dc * 128:(dc + 1) * 128], hT[:, dc, c0:c1],
                                      start=(dc == 0), stop=(dc == 7)), reads=[rwu] + rh, writes=[ru])
                    fi = ft_ctr[0] % 4
                    ft_ctr[0] += 1
                    t = ftmp[:, fi, 0:w]
                    rt = RF[fi]
                    P.op("act", ACT(t, gps[:, 0:w], AF.Silu), reads=[rg], writes=[rt])
                    P.op("dve", TT(aT[:, jj, c0:c1], t, ups[:, 0:w], ALU.mult), reads=[rt, ru], writes=[RA[jj][si]])

                if sub_major_first and (ja, jb) == PARTS[0] and len(subs) == 2 and 2 * npart <= NSLOT:
                    base = ring_state["next"]
                    pcs = [(ring_get(), ring_get()) for _ in range(npart)]
                    for jj in range(npart):
                        (wg, rwg), (wu, rwu) = pcs[jj]
                        gate_up(jj, 0, wg, rwg, wu, rwu)
                    for jj in range(npart):
                        (wg, rwg), (wu, rwu) = pcs[jj]
                        gate_up(jj, 1, wg, rwg, wu, rwu)
                        ring_issue(base + 2 * (jj + 1) + NSLOT)
                        bg_ctr[0] += 1
                        if bg_dma and bg_ctr[0] % 2 == 0:
                            bg_dma.pop(0)()
                else:
                    for jj in range(npart):
                        wg, rwg = ring_get()
                        wu, rwu = ring_get()
                        for si in range(len(subs)):
                            gate_up(jj, si, wg, rwg, wu, rwu)
                        ring_done()
                        if KFFN == "g1":
                            return
                if KFFN == "g":
                    return
                wds = [ring_get() for _ in range(npart)]
                for lb in blocks:
                    n = ncols[lb]
                    for half in range(2):
                        yps, _, ry = psum()
                        for jj in range(npart):
                            wd, rwd = wds[jj]
                            P.op("pe", MM(yps[:n, :], aT[:, jj, lb * 128:lb * 128 + n], wd[:, half * 512:(half + 1) * 512],
                                          start=(jj == 0), stop=(jj == npart - 1)),
                                 reads=[rwd] + [RA[jj][si] for si in sub_of_lb[lb]], writes=[ry])
                        xs = xt[:n, lb, half * 512:(half + 1) * 512]
                        P.op("dve", STT(xs, yps[:n, :], 0.5, xs, ALU.mult, ALU.add), reads=[ry, RX[lb]], writes=[RX[lb]])
                    if on_block_final is not None and (ja, jb) == PARTS[-1]:
                        pend_final.append(lb)
                        if len(pend_final) > 1:
                            on_block_final(pend_final.pop(0))
                if on_block_final is not None and (ja, jb) == PARTS[-1]:
                    while pend_final:
                        on_block_final(pend_final.pop(0))
                if not (skip_last_ring_done and (ja, jb) == PARTS[-1]):
                    ring_done()
                if part_hook:
                    part_hook.pop(0)()
                if KFFN == "d":
                    return

        def gelu2(ps, rps, dst, rdst):
            P.op("act", ACT(dst, ps, AF.Gelu_apprx_tanh), reads=[rps], writes=[rdst])

        def mixer_tail(lb, n, oA, rA, oB, rB, bpb=None, bops=((None, None), (None, None))):
            rd8 = R["dn"]
            for kvh, (o, ro) in enumerate(((oA, rA), (oB, rB))):
                ov = o[:n, 0:260].rearrange("p (g c) -> p g c", g=4)
                P.op("dve", TT(dn8[:n, kvh * 4:(kvh + 1) * 4], ov[:, :, 64], esink[:n, kvh * 4:(kvh + 1) * 4], ALU.add),
                     reads=[ro, R["esink"]], writes=[rd8])
            P.op("dve", RCP(dn8[:n, :], dn8[:n, :]), reads=[rd8], writes=[rd8])
            for kvh, (o, ro) in enumerate(((oA, rA), (oB, rB))):
                ov = o[:n, 0:260].rearrange("p (g c) -> p g c", g=4)[:, :, 0:64]
                dst = CB.mraw[:n, kvh * 256:(kvh + 1) * 256].rearrange("p (g c) -> p g c", g=4)
                bc = dn8[:n, kvh * 4:(kvh + 1) * 4].unsqueeze(2).to_broadcast([n, 4, 64])
                P.op("dve", TT(dst, ov, bc, ALU.mult), reads=[ro, rd8], writes=[CB.R_attn])
            yield
            P.op("dve", TT(CB.m[:n, 0:512], CB.mraw[:n, 0:512], gains[:n, 4, 0:512], ALU.mult),
                 reads=[CB.R_attn, RC], writes=[CB.R_m_a])
            P.op("dve", TT(CB.m[:n, 512:1024], CB.mraw[:n, 512:1024], gains[:n, 4, 512:1024], ALU.mult),
                 reads=[CB.R_gate2, RC], writes=[CB.R_m_g])
            yield
            _, pb, rb = psum(bpb)
            for cc in range(8):
                P.op("pe", TR(pb[:, cc * n:(cc + 1) * n], CB.m[:n, cc * 128:(cc + 1) * 128], ident_b[:n, :n]),
                     reads=[CB.R_m_a, CB.R_m_g, R["ident_b"]], writes=[rb])
            P.op("act", ACT(CB.mT[:, :, 0:n], pb[:, 0:8 * n].rearrange("p (a b) -> p a b", a=8), AF.Copy),
                 reads=[rb], writes=[CB.R_mT])
            yield
            sa, rsa = newstat()
            P.op("act", ACT(CB.mraw[:n, 0:512], CB.mraw[:n, 0:512], AF.Square, accum_out=sa[:n, 0:1]),
                 reads=[CB.R_attn], writes=[CB.R_attn, rsa])
            P.op("act", ACT(CB.mraw[:n, 512:1024], CB.mraw[:n, 512:1024], AF.Square, accum_out=sa[:n, 1:2]),
                 reads=[CB.R_gate2], writes=[CB.R_gate2, rsa])
            P.op("dve", STT(sa[:n, 2:4], sa[:n, 0:2], 1.0 / 512, epsv[:n, 0:2], ALU.mult, ALU.add), reads=[rsa, R["mh"]], writes=[rsa])
            P.op("pool", TT(sa[:n, 2:4], sa[:n, 2:4], mh[:n, 0:2], ALU.pow), reads=[rsa, R["mh"]], writes=[rsa])
            for half in range(2):
                yield
                opa, _, ropa = psum(bops[half][0])
                opg, _, ropg = psum(bops[half][1])
                for cc in range(4):
                    P.op("pe", MM(opa[:n, :], CB.mT[:, cc, 0:n], wout[:, cc, half * 512:(half + 1) * 512],
                                  start=(cc == 0), stop=(cc == 3)), reads=[CB.R_mT, R["wout"]], writes=[ropa])
                for cc in range(4, 8):
                    P.op("pe", MM(opg[:n, :], CB.mT[:, cc, 0:n], wout[:, cc, half * 512:(half + 1) * 512],
                                  start=(cc == 4), stop=(cc == 7)), reads=[CB.R_mT, R["wout"]], writes=[ropg])
                xs = xt[:n, lb, half * 512:(half + 1) * 512]
                P.op("dve", STT(xs, opa[:n, :], sa[:n, 2:3], xs, ALU.mult, ALU.add), reads=[ropa, rsa, RX[lb]], writes=[RX[lb]])
                P.op("dve", STT(xs, opg[:n, :], sa[:n, 3:4], xs, ALU.mult, ALU.add), reads=[ropg, rsa, RX[lb]], writes=[RX[lb]])

        def proj_qk(lb, n, c0, kbuf, rk, want_q=True, perm_q=False, bq=None, bk=None):
            rh = [RH[lb], R["wqk"]]
            if want_q:
                qps, _, rq = psum(bq)
                for cb in range(4):
                    for dc in range(8):
                        P.op("pe", MM(qps[:, cb * n:(cb + 1) * n], wqk[:, dc, cb * 128:(cb + 1) * 128],
                                      hT[:, dc, c0:c0 + n], start=(dc == 0), stop=(dc == 7)), reads=rh, writes=[rq])
                if perm_q:
                    P.op("act", ACT(CB.qT[:, 0:256].rearrange("p (b g i) -> p g b i", b=16, g=4),
                                    qps[:, 0:256].rearrange("p (g b i) -> p g b i", g=4, b=16), AF.Copy),
                         reads=[rq], writes=[CB.R_qT])
                else:
                    P.op("act", ACT(CB.qTz[0:64, 0, :], qps[0:64, 0:512], AF.Copy), reads=[rq], writes=[CB.R_qT])
                    P.op("act", ACT(CB.qTz[64:128, 1, :], qps[64:128, 0:512], AF.Copy), reads=[rq], writes=[CB.R_qT])
            kps, _, rkp = psum(bk)
            for dc in range(8):
                P.op("pe", MM(kps[:, 0:n], wqk[:, dc, 512:640], hT[:, dc, c0:c0 + n], start=(dc == 0), stop=(dc == 7)),
                     reads=rh, writes=[rkp])
            P.op("dve", CP(kbuf[:, 0:n], kps[:, 0:n]), reads=[rkp], writes=[rk])

        def proj_tok(lb, n, c0, bu=None, bg=None, bkv=None):
            ups, _, rups = psum(bu)
            gps, _, rgps = psum(bg)
            kvps, _, rkv = psum(bkv)
            for (ps_, rp_, a, b) in ((ups, rups, 0, 512), (gps, rgps, 512, 1024), (kvps, rkv, 1024, 1280)):
                for dc in range(8):
                    P.op("pe", MM(ps_[:n, 0:b - a], hT[:, dc, c0:c0 + n], wtok[:, dc, a:b], start=(dc == 0), stop=(dc == 7)),
                         reads=[RH[lb], R["wtok"]], writes=[rp_])
            return ups, rups, gps, rgps, kvps, rkv

        def gmlp_front(lb, n, ups, rups, gps, rgps, sample, bm=None):
            gelu2(ups[:n, :], rups, CB.u2f[:n, :], CB.R_u2)
            yield
            gelu2(gps[:n, :], rgps, CB.g2f[:n, :], CB.R_g2)
            yield
            sg, rsg = newstat()
            scr = CB.mraw[:n, 512:1024]
            P.op("act", ACT(scr, CB.g2f[:n, :], AF.Square), reads=[CB.R_g2], writes=[CB.R_gate2])
            P.op("dve", lambda e, sg=sg, scr=scr: e.tensor_reduce(out=sg[:n, 0:4], in_=scr.rearrange("p (g c) -> p g c", g=4),
                                                                  axis=mybir.AxisListType.X, op=ALU.add),
                 reads=[CB.R_gate2], writes=[rsg])
            rstd_from_ss(sg[:n, 0:4], sg[:n, 0:4], rsg, n, 1.0 / 128, EPS)
            dst = gn32 if sample else gnb
            rdst = R["otsb"] if sample else R["gnb"]
            bc = sg[:n, 0:4].unsqueeze(2).to_broadcast([n, 4, 128])
            P.op("dve", TT(CB.g2[:n, :, :], CB.g2[:n, :, :], bc, ALU.mult), reads=[CB.R_g2, rsg], writes=[CB.R_g2])
            P.op("dve", TT(dst[:n, :, :], CB.g2[:n, :, :], vnb[:n, :, :], ALU.mult), reads=[CB.R_g2, RC], writes=[rdst])
            if sample:
                P.op("dve", CP(gnb[:n, :, :], gn32[:n, :, :]), reads=[R["otsb"]], writes=[R["gnb"]])
                P.op("sp", DMA(scv_d, gn32[:n, :, :].rearrange("p a b -> p (a b)")), reads=[R["otsb"]], dma="o_scv")
            yield
            mps, _, rm = psum(bm)
            for G in range(4):
                if sample:
                    P.op("pe", MM(mps[:n, G * 128:(G + 1) * 128], BD[:n, G, :], gnb[:n, G, :]),
                         reads=[R["BD"], R["gnb"]], writes=[rm])
                else:
                    P.op("pe", MM(mps[:n, G * 128:(G + 1) * 128], WtT[:, G, :], gnb[:, G, :]),
                         reads=[R["WtT"], R["gnb"]], writes=[rm])
            bt = bsps if sample else bsp
            gdst = CB.mraw[:n, 512:1024].rearrange("p (g c) -> p g c", g=4)
            bcb = bt[:n, 0:4].unsqueeze(2).to_broadcast([n, 4, 128])
            P.op("dve", TT(gdst, mps[:n, :].rearrange("p (g c) -> p g c", g=4), bcb, ALU.add), reads=[rm, RC], writes=[CB.R_gate2])
            P.op("dve", TT(gdst, gdst, CB.u2[:n, :, :], ALU.mult), reads=[CB.R_gate2, CB.R_u2], writes=[CB.R_gate2])

        eb_ctr = [0]

        def exp_mask(sps, rsp, n, w, ebias_ap, dstPT, rdst, use_flag=False):
            i = eb_ctr[0] % 2
            eb_ctr[0] += 1
            eb_, re_ = ebuf[i], R[f"ebuf{i}"]
            P.op("act", ACT(eb_[:n, 0:w], sps[:n, 0:w], AF.Exp, scale=0.125), reads=[rsp], writes=[re_])
            if use_flag:
                P.op("dve", STT(dstPT[:n, 0:w], eb_[:n, 0:w], flag[:n, 0:1], ebias_ap[:n], ALU.mult, ALU.mult),
                     reads=[re_, RC], writes=[rdst])
            else:
                P.op("dve", TT(dstPT[:n, 0:w], eb_[:n, 0:w], ebias_ap[:n], ALU.mult), reads=[re_, RC], writes=[rdst])

        def mixer_frontA(gb, lb, S):
            n = 128
            c0 = lb * 128
            cur, prev = gb % 3, (gb - 1) % 3
            proj_qk(lb, n, c0, kT[cur], R[f"kT{cur}"], bq=0, bk=1)
            yield
            pi = 0
            pts = {}
            S["pts"] = pts
            for kvh in range(2):
                for kbi, kb in enumerate((prev, cur)):
                    sps, _, rsp = psum((0, 1, 6, 7)[kvh * 2 + kbi])
                    P.op("pe", MM(sps[:, :], kT[kb][:, :], CB.qTz[:, kvh, :], start=True, stop=False),
                         reads=[R[f"kT{kb}"], CB.R_qT], writes=[rsp])
                    if gb == 1 and kbi == 0:
                        btab = bmf[:, kvh * 4:(kvh + 1) * 4, :].rearrange("p a b -> p (a b)")
                    else:
                        btab = bm[:, kbi, kvh * 4:(kvh + 1) * 4, :].rearrange("p a b -> p (a b)")
                    P.op("pe", MM(sps[:, :], ident_b[:, :], btab, start=False, stop=True),
                         reads=[R["ident_b"], R["bm"]], writes=[rsp])
                    P.op("act", ACT(PT[pi][:, :], sps[:, :], AF.Exp, scale=0.125), reads=[rsp], writes=[R[f"PT{pi}"]])
                    pts[(kvh, kbi)] = pi
                    pi += 1
                yield

        def mixer_frontB(gb, lb, S):
            n = 128
            c0 = lb * 128
            cur = gb % 3
            ups, rups, gps, rgps, kvps, rkv = proj_tok(lb, n, c0, bu=2, bg=3, bkv=6)
            yield
            gm = gmlp_front(lb, n, ups, rups, gps, rgps, False, bm=6)
            S["gm"] = gm
            next(gm)
            yield
            next(gm)
            next(gm)
            P.op("act", ACT(Va[cur][:, :, 0:64], kvps[:, 128:256].rearrange("p (k d) -> p k d", k=2), AF.Copy),
                 reads=[rkv], writes=[R[f"Va{cur}"]])
            if gb == 16:
                P.op("act", ACT(kvout[:, :], kvps[:, 0:256], AF.Copy), reads=[rkv], writes=[R["kvout"]])
                P.op("sp", DMA(pk_d, kvout[:, 0:128]), reads=[R["kvout"]], dma="o_pk")
                P.op("sp", DMA(pv_d, kvout[:, 128:256]), reads=[R["kvout"]], dma="o_pv")
            yield

        def mixer_pv(gb, lb, S):
            n = 128
            cur, prev = gb % 3, (gb - 1) % 3
            pts = S["pts"]
            oA, _, rA = psum(4)
            oB, _, rB = psum(5)
            for kvh, (o, ro) in enumerate(((oA, rA), (oB, rB))):
                for g in range(4):
                    for kbi, kb in enumerate((prev, cur)):
                        pj = pts[(kvh, kbi)]
                        P.op("pe", MM(o[:, g * 65:(g + 1) * 65], PT[pj][:, g * 128:(g + 1) * 128], Va[kb][:, kvh, :],
                                      start=(kbi == 0), stop=(kbi == 1)),
                             reads=[R[f"PT{pj}"], R[f"Va{kb}"]], writes=[ro])
            S["tail"] = mixer_tail(lb, n, oA, rA, oB, rB, bpb=6, bops=((4, 5), (6, 7)))

        def mixer_halo(lb):
            n = 128
            c0 = lb * 128
            proj_qk(lb, n, c0, kT[0], R["kT0"], want_q=False)
            kvps, _, rkv = psum()
            for dc in range(8):
                P.op("pe", MM(kvps[:n, 0:128], hT[:, dc, c0:c0 + n], wtok[:, dc, 1152:1280], start=(dc == 0), stop=(dc == 7)),
                     reads=[RH[lb], R["wtok"]], writes=[rkv])
            P.op("act", ACT(Va[0][:, :, 0:64], kvps[:, 0:128].rearrange("p (k d) -> p k d", k=2), AF.Copy),
                 reads=[rkv], writes=[R["Va0"]])

        def mixer_sample_prep():
            allRA = [RA[j][s_] for j in range(6) for s_ in range(2)]
            P.op("pool", DMA(ckb, ck_d.rearrange("b r c -> r b c")), writes=[RF[0], RF[1]], dma="c_ck")
            P.op("pool", DMA(cvraw, cv_d.rearrange("b r c -> r b c")), writes=[RF[2], RF[3]], dma="c_cv")
            P.op("pool", MS(cvb_flat, 1.0), writes=allRA)
            P.op("dve", CP(cvb[:, :, :, 0:64], cvraw.rearrange("p b (k d) -> p b k d", k=2)),
                 reads=[RF[2], RF[3]], writes=allRA)
            for grp in range(2):
                _, pb, rb = psum()
                for bb in range(8):
                    b = grp * 8 + bb
                    P.op("pe", TR(pb[:, bb * 128:(bb + 1) * 128], ckb[:, b, :], ident_b[:, :]),
                         reads=[RF[0], RF[1], R["ident_b"]], writes=[rb])
                P.op("act", ACT(kTc[:, grp * 8:(grp + 1) * 8, :], pb[:, :].rearrange("p (a b) -> p a b", a=8), AF.Copy),
                     reads=[rb], writes=allRA)

        def mixer_sample(lb):
            n = 64
            c0 = lb * 128
            allRA = [RA[j][s_] for j in range(6) for s_ in range(2)]
            KS = os.environ.get("KSAMP", "")
            proj_qk(lb, n, c0, kT[2], R["kT2"], perm_q=True)
            ups, rups, gps, rgps, kvps, rkv = proj_tok(lb, n, c0)
            P.op("act", ACT(Va[2][:n, :, 0:64], kvps[:n, 128:256].rearrange("p (k d) -> p k d", k=2), AF.Copy),
                 reads=[rkv], writes=[R["Va2"]])
            P.op("act", ACT(kvout[:n, :], kvps[:n, 0:256], AF.Copy), reads=[rkv], writes=[R["kvout"]])
            for b in range(16):
                P.op("sp", DMA(sk_d[b, 124:128, :], kvout[b * 4:(b + 1) * 4, 0:128]), reads=[R["kvout"]], dma="o_skn")
                P.op("sp", DMA(sv_d[b, 124:128, :], kvout[b * 4:(b + 1) * 4, 128:256]), reads=[R["kvout"]], dma="o_svn")
            if KS == "s2":
                return
            for _ in gmlp_front(lb, n, ups, rups, gps, rgps, True):
                pass
            if KS == "s3":
                return
            for kvh in range(2):
                scps, _, rsc = psum()
                for b in range(16):
                    P.op("pe", MM(scps[:, b * 16:(b + 1) * 16], kTc[kvh * 64:(kvh + 1) * 64, b, :],
                                  CB.qT[kvh * 64:(kvh + 1) * 64, b * 16:(b + 1) * 16]),
                         reads=allRA + [CB.R_qT], writes=[rsc])
                exp_mask(scps, rsc, 128, 256, ebc[:, kvh * 256:(kvh + 1) * 256], PT[0][:, kvh * 256:(kvh + 1) * 256], R["PT0"])
            if KS == "s4":
                return
            for kvh in range(2):
                snps, _, rsn = psum()
                P.op("pe", MM(snps[:n, 0:256], kT[2][kvh * 64:(kvh + 1) * 64, 0:n], CB.qT[kvh * 64:(kvh + 1) * 64, 0:256]),
                     reads=[R["kT2"], CB.R_qT], writes=[rsn])
                exp_mask(snps, rsn, n, 256, ebn[:, kvh * 256:(kvh + 1) * 256], PT[1][:, kvh * 256:(kvh + 1) * 256], R["PT1"])
            otps, _, rot = psum()
            onps, _, ron = psum()
            for kvh in range(2):
                P.op("pe", MM(onps[0:65, kvh * 256:(kvh + 1) * 256], Va[2][:n, kvh, :], PT[1][:n, kvh * 256:(kvh + 1) * 256]),
                     reads=[R["Va2"], R["PT1"]], writes=[ron])
            for kvh in range(2):
                for b in range(16):
                    col = kvh * 256 + b * 16
                    P.op("pe", MM(otps[0:65, col:col + 16], cvb[:, b, kvh, :], PT[0][:, col:col + 16]),
                         reads=allRA + [R["PT0"]], writes=[rot])
            for kvh in range(2):
                src = otps[0:65, kvh * 256:(kvh + 1) * 256].rearrange("p (b g i) -> p g b i", b=16, g=4)
                dst = otsb[0:65, kvh * 256:(kvh + 1) * 256].rearrange("p (g b i) -> p g b i", g=4, b=16)
                P.op("act", ACT(dst, src, AF.Copy), reads=[rot], writes=[R["otsb"]])
            for kvh in range(2):
                dst = otsb[0:65, kvh * 256:(kvh + 1) * 256].rearrange("p (g b i) -> p g b i", g=4, b=16)
                src = onps[0:65, kvh * 256:(kvh + 1) * 256].rearrange("p (b g i) -> p g b i", b=16, g=4)
                P.op("dve", TT(dst, dst, src, ALU.add), reads=[R["otsb"], ron], writes=[R["otsb"]])
            if KS == "s5":
                return
            oA, _, rA = psum()
            oB, _, rB = psum()
            for kvh, (o, ro) in enumerate(((oA, rA), (oB, rB))):
                for g in range(4):
                    h = kvh * 4 + g
                    P.op("pe", TR(o[:n, g * 65:(g + 1) * 65], otsb[0:65, h * 64:(h + 1) * 64], ident_f[0:65, 0:65]),
                         reads=[R["otsb"], RC], writes=[ro])
            for _ in mixer_tail(lb, n, oA, rA, oB, rB):
                pass

        def final_norm(lb, n, row0):
            stt, rs = newstat()
            i = hb_ctr[0] % 2
            hb_ctr[0] += 1
            P.op("act", ACT(hb[i][:n, :], xt[:n, lb, :], AF.Square, accum_out=stt[:n, 0:1]),
                 reads=[RX[lb]], writes=[RHB[i], rs])
            rstd_from_ss(stt[:n, 0:1], stt[:n, 1:2], rs, n, 1.0 / D, EPS)
            P.op("dve", STT(xt[:n, lb, :], xt[:n, lb, :], stt[:n, 1:2], gains[:n, 3, :], ALU.mult, ALU.mult),
                 reads=[RX[lb], rs, RC], writes=[RX[lb]])
            P.op("sp", DMA(yout_d[row0:row0 + n, :], xt[:n, lb, :]), reads=[RX[lb]], dma=f"o_y{lb}")

        import os
        STOP = int(os.environ.get("KSTOP", "99"))
        for t in range(3):
            if STOP < 99 and t > 0:
                break
            if t >= int(os.environ.get("KTILES", "3")):
                break
            gblocks = list(range(t * 6, t * 6 + 6))
            ncols = {}
            for lb, gb in enumerate(gblocks):
                n = 64 if gb == 17 else 128
                ncols[lb] = n
                if t == 0 or STOP < 99:
                    P.op("sp", DMA(xt[:n, lb, :], xin_d[gb * 128:gb * 128 + n, :]), writes=[RX[lb]], dma=f"x{lb}")
            if STOP < 1:
                break
            if t == 0:
                part_hook.extend([late_init, late_init2])
            ffn(list(range(6)), ncols, 0, sub_major_first=True,
                after_first_norm=(lambda: ring_issue(NSLOT)) if t == 0 else ring_done)
            if t == 0:
                while bg_dma:
                    bg_dma.pop(0)()
                late_init_compute()
            if STOP < 2:
                break
            gens = []
            norm_phase(list(range(len(gblocks))), ncols, 1)
            if 17 in gblocks and STOP >= 4:
                mixer_sample_prep()
            for lb, gb in enumerate(gblocks):
                if STOP < 3 and gb > 0:
                    break
                if STOP < 4 and gb > 1:
                    break
                if gb == 0:
                    CB.set(0)
                    mixer_halo(lb)
                elif gb == 17:
                    pass
                else:
                    gens.append((gb, lb))

            def adv(par, g):
                CB.set(par)
                try:
                    next(g)
                    return True
                except StopIteration:
                    return False

            states = {gb: {} for gb, lb in gens}

            def step(g, par):
                CB.set(par)
                try:
                    next(g)
                except StopIteration:
                    pass

            fronts = {}

            def mk_front(i):
                gb_, lb_ = gens[i]
                fronts[i] = (mixer_frontA(gb_, lb_, states[gb_]), mixer_frontB(gb_, lb_, states[gb_]), gb_ % 2)

            if gens:
                mk_front(0)
                fa, fb, pz = fronts[0]
                for g_ in (fa, fa, fa, fb, fb, fb):
                    step(g_, pz)
            blocks2_ = [lb_ for lb_, gb_ in enumerate(gblocks) if gb_ != 0]
            subsA = subs_of(blocks2_, ncols)[0]
            grpA = [lb_ for lb_ in subsA[2] if lb_ * 128 >= subsA[0]]
            prenormed2 = []
            for i, (gb, lb) in enumerate(gens):
                if i == len(gens) - 1 and len(gens) >= 2 and STOP >= 99 and all(l_ < lb for l_ in grpA):
                    norm_phase(grpA, ncols, 2)
                    prenormed2 = list(grpA)
                Sb = states[gb]
                pb_ = gb % 2
                have_f = i + 1 < len(gens)
                if have_f:
                    mk_front(i + 1)
                    fa, fb, pf = fronts[i + 1]
                nop = iter(())
                if not have_f:
                    fa, fb, pf = nop, nop, 0
                step(fa, pf)
                CB.set(pb_)
                mixer_pv(gb, lb, Sb)
                step(Sb["tail"], pb_)
                step(Sb["gm"], pb_)
                step(Sb["tail"], pb_)
                step(fa, pf)
                step(fa, pf)
                step(Sb["tail"], pb_)
                step(fb, pf)
                step(fb, pf)
                step(fb, pf)
                step(Sb["tail"], pb_)
                step(Sb["tail"], pb_)
                step(Sb["tail"], pb_)
                step(Sb["tail"], pb_)
            if 17 in gblocks and STOP >= 4:
                CB.set(0)
                mixer_sample(gblocks.index(17))
            if STOP < 5:
                break
            blocks2 = [lb for lb, gb in enumerate(gblocks) if gb != 0]

            def block_final(lb, t=t, gblocks=gblocks, ncols=ncols):
                n = ncols[lb]
                row0 = (gblocks[lb] - 1) * 128
                stt, rs = newstat()
                j = junk_ctr[0] % 2
                junk_ctr[0] += 1
                P.op("act", ACT(hb[j][:n, :], xt[:n, lb, :], AF.Square, accum_out=stt[:n, 0:1]),
                     reads=[RX[lb]], writes=[RHB[j], rs])
                rstd_from_ss(stt[:n, 0:1], stt[:n, 1:2], rs, n, 1.0 / D, EPS)
                P.op("dve", STT(xt[:n, lb, :], xt[:n, lb, :], stt[:n, 1:2], gains[:n, 3, :], ALU.mult, ALU.mult),
                     reads=[RX[lb], rs, RC], writes=[RX[lb]])
                P.op("sp", DMA(yout_d[row0:row0 + n, :], xt[:n, lb, :]), reads=[RX[lb]], dma=f"o_y{lb}")
                if t < 2 and STOP >= 99:
                    pend_loads.append(lb)
                    if t == 0 and lb == 1:
                        pend_loads.insert(0, 0)
                    while len(pend_loads) > 1:
                        emit_load(pend_loads.pop(0))

            def emit_load(l2, t=t):
                gb2 = (t + 1) * 6 + l2
                n2 = 64 if gb2 == 17 else 128
                P.op("sp", DMA(xt[:n2, l2, :], xin_d[gb2 * 128:gb2 * 128 + n2, :]), writes=[RX[l2]], dma=f"x{l2}")

            pend_loads = []
            if t == 2:
                part_hook.extend([lambda: None, late_copies])
            ffn(blocks2, ncols, 2, on_block_final=block_final, skip_last_ring_done=(t < 2 and STOP >= 99), prenormed=prenormed2)
            while pend_loads:
                emit_load(pend_loads.pop(0))
            if STOP < 6:
                break
        if STOP < 99:
            for lb in range(1, 6):
                P.op("sp", DMA(yout_d[(lb - 1) * 128:lb * 128, :], xt[:, lb, :]), reads=[RX[lb]], dma=f"o_y{lb}")
        P.emit(nc, st)
    return nc


_SLOPES = (2.0 ** (-8.0 * np.arange(1, 9, dtype=np.float32) / 8)).astype(np.float32)


def _const_tables():
    j = np.arange(128)[:, None]
    i = np.arange(128)[None, :]
    ebm = np.zeros((128, 2, 8, 128), np.float32)
    NEG = -30000.0
    for h in range(8):
        s = np.float64(_SLOPES[h])
        d_prev = 128 + i - j
        ebm[:, 0, h, :] = np.where(j > i, -8.0 * s * d_prev, NEG)
        d_cur = i - j
        ebm[:, 1, h, :] = np.where(j <= i, -8.0 * s * d_cur, NEG)
    r = np.arange(128)[:, None]
    ii = np.arange(4)[None, :]
    ebc = np.zeros((128, 2, 16, 4, 4), np.float32)
    for h in range(8):
        s = np.float64(_SLOPES[h])
        v = np.where(r >= ii + 1, np.exp(-s * (128 + ii - r)), 0.0)
        ebc[:, h // 4, :, h % 4, :] = v[:, None, :]
    ebn = np.zeros((16, 4, 2, 16, 4, 4), np.float32)
    jj = np.arange(4)[:, None]
    i4 = np.arange(4)[None, :]
    for h in range(8):
        s = np.float64(_SLOPES[h])
        v = np.where(jj <= i4, np.exp(-s * (i4 - jj)), 0.0)
        for b in range(16):
            ebn[b, :, h // 4, b, h % 4, :] = v
    tril = (j <= i).astype(np.float32)
    masks = np.zeros((16, 4, 16, 4), np.float32)
    for b in range(16):
        masks[b, :, b, :] = (jj <= i4)
    return (ebm.reshape(128, -1), ebc.reshape(128, 512), ebn.reshape(64, 512), tril, masks.reshape(64, 64))


def _ffn_pieces(wg, wu, wd):
    wgp = np.ascontiguousarray(wg.reshape(8, 128, NJ, 128).transpose(2, 1, 0, 3)).reshape(NJ, 128, 1024)
    wup = np.ascontiguousarray(wu.reshape(8, 128, NJ, 128).transpose(2, 1, 0, 3)).reshape(NJ, 128, 1024)
    wdp = wd.reshape(NJ, 128, 1024)
    out = []
    for (a, b) in PARTS:
        for j in range(a, b):
            out.append(wgp[j])
            out.append(wup[j])
        for j in range(a, b):
            out.append(wdp[j])
    return out


_NC_CACHE = {}


def kernel(x_prompt, x_sample, cache_k, cache_v, ffn1_norm, ffn1_w_gate, ffn1_w_up, ffn1_w_down,
           mix_norm, w_in, attn_sinks, gmlp_v_norm, gmlp_w_spatial, gmlp_b_spatial,
           attn_out_norm, gmlp_out_norm, w_out, ffn2_norm, ffn2_w_gate, ffn2_w_up, ffn2_w_down,
           final_norm):
    f = lambda a: np.asarray(a, dtype=np.float32)
    x_prompt, x_sample, cache_k, cache_v = f(x_prompt), f(x_sample), f(cache_k), f(cache_v)
    w_in0 = f(w_in)[0]
    ebm, ebc, ebn, tril, masks = _const_tables()

    wffn = np.stack(_ffn_pieces(f(ffn1_w_gate)[0], f(ffn1_w_up)[0], f(ffn1_w_down)[0]) +
                    _ffn_pieces(f(ffn2_w_gate)[0], f(ffn2_w_up)[0], f(ffn2_w_down)[0]), axis=0)
    qperm = np.concatenate([np.concatenate([np.arange(j * 64, (j + 1) * 64), np.arange((4 + j) * 64, (5 + j) * 64)])
                            for j in range(4)])
    cols_qk = np.concatenate([qperm, np.arange(512, 640)])
    cols_tok = np.concatenate([np.arange(768, 1280), np.arange(1280, 1792), np.arange(512, 768)])

    def lay(wmat):
        C = wmat.shape[1]
        return np.ascontiguousarray(wmat.reshape(8, 128, C).transpose(1, 0, 2)).reshape(128, 8 * C)

    wqk = lay(w_in0[:, cols_qk])
    wtok = lay(w_in0[:, cols_tok])
    wout = lay(f(w_out)[0])
    gout = np.concatenate([f(attn_out_norm)[0], f(gmlp_out_norm)[0]])
    gains = np.stack([np.broadcast_to(g, (128, D)) for g in
                      (f(ffn1_norm)[0], f(mix_norm)[0], f(ffn2_norm)[0], f(final_norm), gout)], axis=0)
    gains = np.ascontiguousarray(gains)
    vnb = np.ascontiguousarray(np.broadcast_to(f(gmlp_v_norm)[0].reshape(1, 512), (128, 512)))
    wsp_full = f(gmlp_w_spatial)[0]
    wsp = np.ascontiguousarray(wsp_full.transpose(2, 0, 1)).reshape(128, 512)
    w4 = wsp_full[:, 0:4, 0:4]
    wsps = np.ascontiguousarray(np.broadcast_to(w4.transpose(2, 0, 1)[None, :, :, None, :], (16, 4, 4, 16, 4))).reshape(64, 256)
    bsp = np.ascontiguousarray(f(gmlp_b_spatial)[0].T)
    bsps = np.ascontiguousarray(np.tile(bsp[0:4], (16, 1)))
    sinks = np.ascontiguousarray(np.broadcast_to(f(attn_sinks)[0].reshape(1, 8), (128, 8)))
    ident = np.eye(128, dtype=np.float32)

    bm_prev = np.ascontiguousarray(ebm.reshape(128, 2, 1024)[:, 0, :])
    bm_none = np.full((128, 1024), -30000.0, np.float32)
    in_maps = []
    for c in range(NCORES):
        b, q = c // 4, c % 4
        main = x_prompt[b, q * 2048:(q + 1) * 2048]
        halo = x_prompt[b, q * 2048 - 128:q * 2048] if q > 0 else np.zeros((128, D), np.float32)
        samp = x_sample[c * 16:(c + 1) * 16].reshape(64, D)
        xin = np.concatenate([halo, main, samp], axis=0)
        in_maps.append({
            "xin": np.ascontiguousarray(xin), "wffn": wffn, "wqk": wqk, "wtok": wtok, "wout": wout,
            "gains": gains, "vnb": vnb, "wsp": wsp, "tril": tril, "wsps": wsps, "masks": masks,
            "bsp": bsp, "bsps": bsps, "sinks": sinks,
            "flag": np.full((128, 1), 1.0 if q > 0 else 0.0, np.float32),
            "bm": ebm, "bmf": (bm_prev if q > 0 else bm_none), "ebc": ebc, "ebn": ebn, "ident": ident,
            "ck": np.ascontiguousarray(cache_k[0, c * 16:(c + 1) * 16].reshape(16, 128, 128)),
            "cv": np.ascontiguousarray(cache_v[0, c * 16:(c + 1) * 16].reshape(16, 128, 128)),
        })
    if "nc" not in _NC_CACHE:
        _NC_CACHE["nc"] = build_program()
    nc = _NC_CACHE["nc"]
    ncr = int(os.environ.get("KCORES", NCORES))
    res = run_bass_kernel_spmd(nc, in_maps[:ncr], core_ids=list(range(ncr)))
    rs = list(res.results)
    while len(rs) < NCORES:
        rs.append(rs[0])
    y_prompt = np.stack([np.concatenate([rs[b * 4 + q]["yout"][0:2048] for q in range(4)], axis=0) for b in range(2)], axis=0)
    y_sample = np.concatenate([rs[c]["yout"][2048:2112].reshape(16, 4, D) for c in range(NCORES)], axis=0)
    prompt_k = np.stack([rs[b * 4 + 3]["pk"].reshape(128, 2, 64) for b in range(2)], axis=0)[None]
    prompt_v = np.stack([rs[b * 4 + 3]["pv"].reshape(128, 2, 64) for b in range(2)], axis=0)[None]
    sample_k = np.concatenate([rs[c]["sk"].reshape(16, 128, 2, 64) for c in range(NCORES)], axis=0)[None]
    sample_v = np.concatenate([rs[c]["sv"].reshape(16, 128, 2, 64) for c in range(NCORES)], axis=0)[None]
    scv = np.concatenate([rs[c]["scv"].reshape(16, 4, 512) for c in range(NCORES)], axis=0)[None]
    return (y_prompt.astype(np.float32), y_sample.astype(np.float32), prompt_k.astype(np.float32),
            prompt_v.astype(np.float32), sample_k.astype(np.float32), sample_v.astype(np.float32),
            scv.astype(np.float32))
```
